# Optimizing a Trainium2 kernel written in Bass

```python
import math
import jax, jax.numpy as jnp
from jax import lax
import numpy as np


D_MODEL = 2048
BATCH = 8
SEQ = 4096
DEPTH = 1
DEC_BATCH = 4
DEC_SEQ = 8192
PAST_LEN = 128

HEAD_DIM = 128
HEADS_PER_GROUP = 4
ATTN_GROUPS = ((128, 1), (512, 4), (2048, 16))
N_ATTN_HEADS = HEADS_PER_GROUP * len(ATTN_GROUPS)
ATTN_WIDTH = N_ATTN_HEADS * HEAD_DIM
ROT_DIM = HEAD_DIM // 4
ROPE_THETA = 500000.0
NEG_INF = -1e30
SSM_WIDTH = D_MODEL // 2
SSM_CH = 16
SSM_GROUPS = SSM_WIDTH // SSM_CH
SSM_STATE = 64
DT_MIN = 0.001
DT_MAX = 0.1
D_FF = 4 * D_MODEL
PLE_DIM = 256
ALPHA = (2.0 * DEPTH) ** 0.25
BETA = (8.0 * DEPTH) ** -0.25
LN_EPS = 1e-5
IN_COLS = 3 * ATTN_WIDTH + SSM_WIDTH + 2 * D_MODEL

kernel_name = "hybrid_dilated_attn_s5_encoder"


def layer_norm(x, g, b):
    xf = x.astype(jnp.float32)
    mu = jnp.mean(xf, axis=-1, keepdims=True)
    var = jnp.mean(jnp.square(xf - mu), axis=-1, keepdims=True)
    y = (xf - mu) * lax.rsqrt(var + LN_EPS) * g.astype(jnp.float32) + b.astype(jnp.float32)
    return y.astype(x.dtype)


def partial_rotary(t, pos):
    inv_freq = ROPE_THETA ** (-jnp.arange(0, ROT_DIM, 2, dtype=jnp.float32) / ROT_DIM)
    ang = pos[:, None] * inv_freq[None, :]
    cos = jnp.cos(ang)[None, :, None, :]
    sin = jnp.sin(ang)[None, :, None, :]
    tr = t[..., :ROT_DIM].astype(jnp.float32)
    x1, x2 = tr[..., : ROT_DIM // 2], tr[..., ROT_DIM // 2:]
    rot = jnp.concatenate([x1 * cos - x2 * sin, x2 * cos + x1 * sin], axis=-1).astype(t.dtype)
    return jnp.concatenate([rot, t[..., ROT_DIM:]], axis=-1)


def dilated_window_attention(q, k, v, window, dilation):
    b, s, h, e = q.shape
    half = window // dilation // 2
    blk = half
    L = s // dilation
    nb = -(-L // blk)
    lp = nb * blk

    def residues(t):
        return t.reshape(b, L, dilation, h, e).transpose(0, 2, 1, 3, 4)

    qb = jnp.pad(residues(q), ((0, 0), (0, 0), (0, lp - L), (0, 0), (0, 0)))
    qb = qb.reshape(b, dilation, nb, blk, h, e)

    def key_blocks(t):
        tp = jnp.pad(residues(t), ((0, 0), (0, 0), (blk, lp - L + blk), (0, 0), (0, 0)))
        tp = tp.reshape(b, dilation, nb + 2, blk, h, e)
        return jnp.concatenate([tp[:, :, :-2], tp[:, :, 1:-1], tp[:, :, 2:]], axis=3)

    kb = key_blocks(k)
    vb = key_blocks(v)
    qi = jnp.arange(blk)[:, None]
    kj = jnp.arange(3 * blk)[None, :]
    rel = kj - blk - qi
    kpos = jnp.arange(nb)[:, None, None] * blk + kj[None] - blk
    mask = (jnp.abs(rel) <= half)[None] & (kpos >= 0) & (kpos < L)

    scores = jnp.einsum('bdnqhe,bdnkhe->bdnhqk', qb.astype(jnp.float32), kb.astype(jnp.float32))
    scores = scores * (1.0 / math.sqrt(e))
    scores = jnp.where(mask[None, None, :, None], scores, NEG_INF)
    m = jnp.max(scores, axis=-1, keepdims=True)
    pr = jnp.exp(scores - m)
    den = jnp.sum(pr, axis=-1, keepdims=True)
    out = jnp.einsum('bdnhqk,bdnkhe->bdnqhe', pr, vb.astype(jnp.float32))
    out = out / den.transpose(0, 1, 2, 4, 3, 5)
    lse = (m + jnp.log(den))[..., 0].transpose(0, 1, 2, 4, 3)

    out = out.reshape(b, dilation, lp, h, e)[:, :, :L].transpose(0, 2, 1, 3, 4).reshape(b, s, h, e)
    lse = lse.reshape(b, dilation, lp, h)[:, :, :L].transpose(0, 2, 1, 3).reshape(b, s, h)
    return out, lse


def attention_mixer(q, k, v):
    b, s = q.shape[:2]
    outs, lses = [], []
    for gi, (window, dilation) in enumerate(ATTN_GROUPS):
        sl = slice(gi * HEADS_PER_GROUP, (gi + 1) * HEADS_PER_GROUP)
        o, l = dilated_window_attention(q[:, :, sl], k[:, :, sl], v[:, :, sl], window, dilation)
        outs.append(o)
        lses.append(l)
    weights = jax.nn.softmax(jnp.stack(lses, axis=0), axis=0)
    merged = jnp.stack(outs, axis=0) * weights[..., None]
    return jnp.moveaxis(merged, 0, 2).reshape(b, s, ATTN_WIDTH)


def _linear_recurrence(e1, e2):
    a1, b1 = e1
    a2, b2 = e2
    return a1 * a2, a2 * b1 + b2


def s5_direction(u, lam_re, lam_im, log_dt, b_re, b_im, c_re, c_im, reverse):
    lam = lax.complex(jnp.minimum(lam_re.astype(jnp.float32), -1e-4), lam_im.astype(jnp.float32))
    dt = jnp.exp(log_dt.astype(jnp.float32))[:, None]
    a_bar = jnp.exp(lam * dt)
    b_mat = lax.complex(b_re.astype(jnp.float32), b_im.astype(jnp.float32))
    b_bar = ((a_bar - 1.0) / lam)[..., None] * b_mat
    bu = jnp.einsum('bsgc,gpc->sbgp', u.astype(jnp.complex64), b_bar)
    a = jnp.broadcast_to(a_bar, (u.shape[1], 1) + a_bar.shape)
    _, hs = lax.associative_scan(_linear_recurrence, (a, bu), axis=0, reverse=reverse)
    c_mat = lax.complex(c_re.astype(jnp.float32), c_im.astype(jnp.float32))
    return jnp.einsum('sbgp,gcp->bsgc', hs, c_mat).real


def mixing_block(x, w_in, lam_re, lam_im, log_dt, b_re, b_im, c_re, c_im, d_skip, w_glu, b_glu,
                 w_att_o, w_ssm_o, w_out):
    b, s, _ = x.shape
    proj = x @ w_in
    a3 = 3 * ATTN_WIDTH
    q, k, v, u, g_att, g_ssm = jnp.split(
        proj, [ATTN_WIDTH, 2 * ATTN_WIDTH, a3, a3 + SSM_WIDTH, a3 + SSM_WIDTH + D_MODEL], axis=-1)
    pos = jnp.arange(s, dtype=jnp.float32)
    q = partial_rotary(q.reshape(b, s, N_ATTN_HEADS, HEAD_DIM), pos)
    k = partial_rotary(k.reshape(b, s, N_ATTN_HEADS, HEAD_DIM), pos)
    v = v.reshape(b, s, N_ATTN_HEADS, HEAD_DIM)
    att = attention_mixer(q, k, v).astype(x.dtype)

    uf = u.astype(jnp.float32)
    ug = uf.reshape(b, s, SSM_GROUPS, SSM_CH)
    y = (s5_direction(ug, lam_re[0], lam_im[0], log_dt[0], b_re[0], b_im[0], c_re[0], c_im[0], False)
         + s5_direction(ug, lam_re[1], lam_im[1], log_dt[1], b_re[1], b_im[1], c_re[1], c_im[1], True))
    y = jax.nn.gelu(y.reshape(b, s, SSM_WIDTH) + d_skip.astype(jnp.float32) * uf)
    ssm = (y * jax.nn.sigmoid(y @ w_glu.astype(jnp.float32) + b_glu.astype(jnp.float32))).astype(x.dtype)

    merged = jax.nn.sigmoid(g_att) * (att @ w_att_o) + jax.nn.sigmoid(g_ssm) * (ssm @ w_ssm_o)
    return merged @ w_out


def encoder_trunk(x, p, ln_emb_g, ln_emb_b, w_in, ssm_lam_re, ssm_lam_im, ssm_log_dt, ssm_b_re, ssm_b_im,
                  ssm_c_re, ssm_c_im, ssm_d, w_glu, b_glu, w_att_o, w_ssm_o, w_out, ln1_g, ln1_b,
                  w_up, w_down, w_ple_gate, w_ple_proj, ln2_g, ln2_b):
    x = layer_norm(x, ln_emb_g, ln_emb_b)
    for l in range(DEPTH):
        mix = mixing_block(x, w_in[l], ssm_lam_re[l], ssm_lam_im[l], ssm_log_dt[l], ssm_b_re[l], ssm_b_im[l],
                           ssm_c_re[l], ssm_c_im[l], ssm_d[l], w_glu[l], b_glu[l], w_att_o[l], w_ssm_o[l], w_out[l])
        x = layer_norm(ALPHA * x + mix, ln1_g[l], ln1_b[l])
        ffn = jnp.square(jax.nn.relu(x @ w_up[l])) @ w_down[l]
        ple = jax.nn.sigmoid(x @ w_ple_gate[l]) * (p[l] @ w_ple_proj[l])
        x = layer_norm(ALPHA * x + ffn + ple, ln2_g[l], ln2_b[l])
    return x


def setup_inputs(seed: int = 0) -> dict:
    key = jax.random.key(seed)
    ks = jax.random.split(key, 32)
    f32 = jnp.float32

    def nrm(k, shape, scale):
        return jax.random.normal(k, shape, f32) * scale

    col_scale = jnp.concatenate([
        jnp.ones((2 * ATTN_WIDTH,), f32), jnp.full((ATTN_WIDTH,), BETA, f32),
        jnp.ones((SSM_WIDTH + 2 * D_MODEL,), f32)])
    w_in = nrm(ks[4], (DEPTH, D_MODEL, IN_COLS), D_MODEL ** -0.5) * col_scale
    n_idx = jnp.arange(SSM_STATE, dtype=f32)
    lam_shape = (DEPTH, 2, SSM_GROUPS, SSM_STATE)
    ssm_lam_re = -0.5 + nrm(ks[5], lam_shape, 0.01)
    ssm_lam_im = math.pi * n_idx + nrm(ks[6], lam_shape, 0.01)
    ssm_log_dt = math.log(DT_MIN) + jax.random.uniform(ks[7], (DEPTH, 2, SSM_GROUPS), f32) * (
        math.log(DT_MAX) - math.log(DT_MIN))
    b_shape = (DEPTH, 2, SSM_GROUPS, SSM_STATE, SSM_CH)
    c_shape = (DEPTH, 2, SSM_GROUPS, SSM_CH, SSM_STATE)
    return {
        "x_prompt": jax.random.normal(ks[0], (BATCH, SEQ, D_MODEL), f32),
        "x_sample": jax.random.normal(ks[1], (DEC_BATCH, DEC_SEQ, D_MODEL), f32),
        "p_prompt": jax.random.normal(ks[2], (DEPTH, BATCH, SEQ, PLE_DIM), f32),
        "p_sample": jax.random.normal(ks[3], (DEPTH, DEC_BATCH, DEC_SEQ, PLE_DIM), f32),
        "ln_emb_g": 1.0 + nrm(ks[8], (D_MODEL,), 0.02),
        "ln_emb_b": nrm(ks[9], (D_MODEL,), 0.02),
        "w_in": w_in,
        "ssm_lam_re": ssm_lam_re,
        "ssm_lam_im": ssm_lam_im,
        "ssm_log_dt": ssm_log_dt,
        "ssm_b_re": nrm(ks[10], b_shape, (0.5 / SSM_CH) ** 0.5),
        "ssm_b_im": nrm(ks[11], b_shape, (0.5 / SSM_CH) ** 0.5),
        "ssm_c_re": nrm(ks[12], c_shape, (0.5 / SSM_STATE) ** 0.5),
        "ssm_c_im": nrm(ks[13], c_shape, (0.5 / SSM_STATE) ** 0.5),
        "ssm_d": nrm(ks[14], (DEPTH, SSM_WIDTH), 1.0),
        "w_glu": nrm(ks[15], (DEPTH, SSM_WIDTH, SSM_WIDTH), SSM_WIDTH ** -0.5),
        "b_glu": nrm(ks[16], (DEPTH, SSM_WIDTH), 0.02),
        "w_att_o": nrm(ks[17], (DEPTH, ATTN_WIDTH, D_MODEL), BETA * ATTN_WIDTH ** -0.5),
        "w_ssm_o": nrm(ks[18], (DEPTH, SSM_WIDTH, D_MODEL), BETA * SSM_WIDTH ** -0.5),
        "w_out": nrm(ks[19], (DEPTH, D_MODEL, D_MODEL), BETA * D_MODEL ** -0.5),
        "ln1_g": 1.0 + nrm(ks[20], (DEPTH, D_MODEL), 0.02),
        "ln1_b": nrm(ks[21], (DEPTH, D_MODEL), 0.02),
        "w_up": nrm(ks[22], (DEPTH, D_MODEL, D_FF), BETA * D_MODEL ** -0.5),
        "w_down": nrm(ks[23], (DEPTH, D_FF, D_MODEL), BETA * D_FF ** -0.5),
        "w_ple_gate": nrm(ks[24], (DEPTH, D_MODEL, D_MODEL), D_MODEL ** -0.5),
        "w_ple_proj": nrm(ks[25], (DEPTH, PLE_DIM, D_MODEL), PLE_DIM ** -0.5),
        "ln2_g": 1.0 + nrm(ks[26], (DEPTH, D_MODEL), 0.02),
        "ln2_b": nrm(ks[27], (DEPTH, D_MODEL), 0.02),
    }


def reference(x_prompt, x_sample, p_prompt, p_sample, ln_emb_g, ln_emb_b, w_in, ssm_lam_re, ssm_lam_im,
              ssm_log_dt, ssm_b_re, ssm_b_im, ssm_c_re, ssm_c_im, ssm_d, w_glu, b_glu, w_att_o, w_ssm_o,
              w_out, ln1_g, ln1_b, w_up, w_down, w_ple_gate, w_ple_proj, ln2_g, ln2_b):
    y_prompt = encoder_trunk(x_prompt, p_prompt, ln_emb_g, ln_emb_b, w_in, ssm_lam_re, ssm_lam_im, ssm_log_dt,
                             ssm_b_re, ssm_b_im, ssm_c_re, ssm_c_im, ssm_d, w_glu, b_glu, w_att_o, w_ssm_o,
                             w_out, ln1_g, ln1_b, w_up, w_down, w_ple_gate, w_ple_proj, ln2_g, ln2_b)
    y_sample = encoder_trunk(x_sample, p_sample, ln_emb_g, ln_emb_b, w_in, ssm_lam_re, ssm_lam_im, ssm_log_dt,
                             ssm_b_re, ssm_b_im, ssm_c_re, ssm_c_im, ssm_d, w_glu, b_glu, w_att_o, w_ssm_o,
                             w_out, ln1_g, ln1_b, w_up, w_down, w_ple_gate, w_ple_proj, ln2_g, ln2_b)
    return (y_prompt, y_sample)
```

```python
import math
from contextlib import ExitStack

import ml_dtypes
import numpy as np

import concourse.bass as bass
import concourse.mybir as mybir
from concourse.bass_utils import run_bass_kernel_spmd

F32 = mybir.dt.float32
BF16 = mybir.dt.bfloat16
AF = mybir.ActivationFunctionType
ALU = mybir.AluOpType
AX = mybir.AxisListType

D = 2048
DFF = 8192
ALPHA = 2.0 ** 0.25
LN_EPS = 1e-5
NH = 12
DIL = (1, 4, 16)
SCALE = 1.0 / math.sqrt(128.0)
TWO_PI = 2.0 * math.pi


class Sched:
    EPOCH = 30000

    def __init__(self, nc, st, nds=24):
        self.nc = nc
        self.st = st
        self.E = {}
        for nm, e in (("pe", nc.tensor), ("act", nc.scalar), ("dve", nc.vector),
                      ("pool", nc.gpsimd), ("sp", nc.sync)):
            self.E[nm] = dict(e=e, sems=[st.enter_context(nc.semaphore("s_%s0" % nm))], n=0, waited={})
        self.ds = [st.enter_context(nc.semaphore("d%d" % i)) for i in range(nds)]
        self.dn = [0] * nds
        self.rr = 0
        self.R = {}

    def _res(self, k):
        r = self.R.get(k)
        if r is None:
            r = [{}, {}]
            self.R[k] = r
        return r

    def _wait(self, en, key, val):
        E = self.E[en]
        if key[0] == "e" and key[1] == en and en == "pe":
            return
        if E["waited"].get(key, 0) >= val:
            return
        if key[0] == "e":
            sem = self.E[key[1]]["sems"][key[2]]
        else:
            sem = self.ds[key[1]]
        E["e"].wait_ge(sem, val)
        E["waited"][key] = val

    def _deps(self, en, reads, writes):
        deps = {}

        def add(k, v):
            if deps.get(k, 0) < v:
                deps[k] = v
        for r in reads:
            for k, v in self._res(r)[0].items():
                add(k, v)
        for w in writes:
            W, Rd = self._res(w)
            for k, v in W.items():
                add(k, v)
            for k, v in Rd.items():
                add(k, v)
        for k, v in deps.items():
            self._wait(en, k, v)

    def _reg(self, key, val, reads, writes):
        for w in writes:
            self.R[w] = [{key: val}, {}]
        for r in reads:
            rd = self._res(r)[1]
            if rd.get(key, 0) < val:
                rd[key] = val

    def _inc(self, en, ins):
        E = self.E[en]
        E["n"] += 1
        ep = (E["n"] - 1) // self.EPOCH
        if ep >= len(E["sems"]):
            E["sems"].append(self.st.enter_context(self.nc.semaphore("s_%s%d" % (en, ep))))
        val = E["n"] - ep * self.EPOCH
        ins.then_inc(E["sems"][ep], 1)
        return ("e", en, ep), val

    def op(self, en, fn, reads=(), writes=()):
        writes = list(writes) + [r for r in reads if isinstance(r, tuple) and r[0] in ("psf", "psb")]
        self._deps(en, reads, writes)
        ins = fn(self.E[en]["e"])
        key, val = self._inc(en, ins)
        self._reg(key, val, reads, writes)

    def mm(self, out, pairs, reads, writes, start=True, stop=True):
        self._deps("pe", reads, writes)
        pe = self.E["pe"]["e"]
        n = len(pairs)
        ins = None
        for i, (l, r) in enumerate(pairs):
            ins = pe.matmul(out, lhsT=l, rhs=r, start=(start and i == 0), stop=(stop and i == n - 1))
        key, val = self._inc("pe", ins)
        self._reg(key, val, reads, writes)

    def tr(self, outs_ins, ident, reads, writes):
        self._deps("pe", reads, writes)
        pe = self.E["pe"]["e"]
        ins = None
        for o, i in outs_ins:
            ins = pe.transpose(o, i, ident)
        key, val = self._inc("pe", ins)
        self._reg(key, val, reads, writes)

    def dma(self, en, out, in_, reads=(), writes=(), **kw):
        self._deps(en, reads, writes)
        idx = self.rr
        self.rr = (self.rr + 1) % len(self.ds)
        if self.dn[idx] > 0:
            self._wait(en, ("d", idx), self.dn[idx])
        self.E[en]["e"].dma_start(out=out, in_=in_, **kw).then_inc(self.ds[idx], 16)
        self.dn[idx] += 16
        self._reg(("d", idx), self.dn[idx], reads, writes)

    def barrier(self):
        mx = {}
        for k, r in self.R.items():
            for d in r:
                for kk, v in d.items():
                    if mx.get(kk, 0) < v:
                        mx[kk] = v
        for en in self.E:
            for kk, v in mx.items():
                self._wait(en, kk, v)

    def finish(self):
        for k, r in list(self.R.items()):
            for kk, v in list(r[0].items()) + list(r[1].items()):
                self._wait("sp", kk, v)


def _tile_fm(w):
    K, N = w.shape
    return np.ascontiguousarray(w.reshape(K // 128, 128, N // 128, 128).transpose(2, 1, 0, 3)).reshape(-1)


def _tile_tm(w, cw=512):
    K, N = w.shape
    return np.ascontiguousarray(w.reshape(K // 128, 128, N // cw, cw).transpose(2, 1, 0, 3)).reshape(-1)


def _tile_tm_kg(w, kg=4, cw=512):
    K, N = w.shape
    kc = K // 128 // kg
    return np.ascontiguousarray(w.reshape(kg, kc, 128, N // cw, cw).transpose(3, 0, 2, 1, 4)).reshape(-1)


WSPEC = [
    ("qk", "fm", 16, 24), ("v", "tm", 16, 3), ("u", "tm", 16, 2), ("gate", "fm", 16, 32),
    ("atto", "fm", 12, 16), ("ssmo", "fm", 8, 16), ("out", "tm", 16, 4), ("up", "fm", 16, 64),
    ("down", "tmkg", 16, 16), ("pg", "tm", 16, 4), ("pp", "tm", 2, 4), ("glu", "fm", 8, 8),
]


def _wlayout():
    off = 0
    lay = {}
    for name, kind, kc, nt in WSPEC:
        cw = 128 if kind == "fm" else 512
        per = 128 * kc * cw
        lay[name] = (off, kc, cw, nt, per)
        off += per * nt
    return lay, off


def _host_weights(w_in, w_att_o, w_ssm_o, w_out, w_up, w_down, w_ple_gate, w_ple_proj, w_glu):
    parts = [
        _tile_fm(w_in[:, 0:3072]), _tile_tm(w_in[:, 3072:4608]), _tile_tm(w_in[:, 4608:5632]),
        _tile_fm(w_in[:, 5632:9728]), _tile_fm(w_att_o), _tile_fm(w_ssm_o), _tile_tm(w_out),
        _tile_fm(w_up), _tile_tm_kg(w_down), _tile_tm(w_ple_gate), _tile_tm(w_ple_proj), _tile_fm(w_glu),
    ]
    return np.concatenate(parts)


def _consts(NT):
    c = {}
    c["identb"] = np.eye(128, dtype=np.float32).astype(ml_dtypes.bfloat16)
    c["identf"] = np.eye(128, dtype=np.float32)
    pm = np.zeros((128, 128), np.float32)
    for j in range(16):
        pm[16 + j, j] = -1.0
        pm[j, 16 + j] = 1.0
    c["pm"] = pm.astype(ml_dtypes.bfloat16)
    k = np.arange(128)[:, None]
    q = np.arange(128)[None, :]
    am = np.zeros((128, 3, 128), np.float32)
    am[:, 0, :] = (k - q >= 64)
    am[:, 1, :] = (np.abs(k - q) <= 64)
    am[:, 2, :] = (q - k >= 64)
    c["amask"] = am.astype(ml_dtypes.bfloat16)
    s = np.arange(128) // 16
    mf = (s[None, :] >= s[:, None]).astype(np.float32)
    mb = (s[:, None] >= s[None, :]).astype(np.float32)
    c["maskf"] = mf
    c["maskb"] = mb
    return c


def _rope(pos):
    inv = (500000.0 ** (-np.arange(0, 32, 2, dtype=np.float32) / np.float32(32))).astype(np.float32)
    ang = (pos[None, :].astype(np.float32) * inv[:, None]).astype(np.float32)
    cos = np.cos(ang.astype(np.float64)).astype(np.float32)
    sin = np.sin(ang.astype(np.float64)).astype(np.float32)
    return np.concatenate([cos, cos], 0), np.concatenate([sin, sin], 0)


def build(NT, debug=False, stop_after=None):
    nc = bass.Bass("TRN2", target_bir_lowering=False)
    NTT = NT // 512
    NSUP = NT // 8
    NBLK = NSUP // 64
    lay, WTOT = _wlayout()

    def din(name, shape, dt=F32):
        return nc.dram_tensor(name, list(shape), dt, kind="ExternalInput").ap()

    def dscr(name, shape, dt):
        return nc.dram_tensor(name, list(shape), dt, kind=("ExternalOutput" if debug else "Internal")).ap()

    x_in = din("x", [NT, D])
    p_in = din("pp_in", [NT, 256])
    wcat = din("wcat", [WTOT])
    identb_d = din("identb", [128, 128], BF16)
    identf_d = din("identf", [128, 128])
    pm_d = din("pm", [128, 128], BF16)
    amask_d = din("amask", [128, 3, 128], BF16)
    amaskl_d = din("amaskl", [128, 3, 128], BF16)
    maskf_d = din("maskf", [128, 128])
    maskb_d = din("maskb", [128, 128])
    ropec_d = din("ropec", [32, NT])
    ropes_d = din("ropes", [32, NT])
    lk_d = din("lk", [128, NBLK + 1])
    lng_d = din("lng", [3, 128, D])
    lnb_d = din("lnb", [3, 128, D])
    lamre_d = din("lamre", [128, 128])
    lamim_d = din("lamim", [128, 128])
    logdt_d = din("logdt", [128, 128])
    bre_d = din("bre", [128, 2, 64, 16])
    bim_d = din("bim", [128, 2, 64, 16])
    cre_d = din("cre", [128, 2, 64, 16])
    cim_d = din("cim", [128, 2, 64, 16])
    dsk_d = din("dsk", [128, 64])
    bglu_d = din("bglu", [128, 8])
    y_out = nc.dram_tensor("y", [NT, D], F32, kind="ExternalOutput").ap()

    wb = dscr("wb", [WTOT], BF16)
    qk_scr = dscr("qk_scr", [24, 128, NT], BF16)
    v_scr = dscr("v_scr", [NT, 1536], BF16)
    u_scr = dscr("u_scr", [NSUP, 64, 8, 16], BF16)
    hb_scr = dscr("hb_scr", [128, NSUP + 1, 64], BF16)
    ssm_scr = dscr("ssm_scr", [8, 128, NT], BF16)
    nd_scr = dscr("nd_scr", [NT, NH, 129], F32)

    st = ExitStack()
    with st:
        S = Sched(nc, st)

        def sb(name, shape, dt=F32):
            return st.enter_context(nc.sbuf_tensor(name, list(shape), dt))

        psf = [st.enter_context(nc.psum_tensor("psf%d" % i, [128, 512], F32)) for i in range(6)]
        psb = [st.enter_context(nc.psum_tensor("psb%d" % i, [128, 1024], BF16)) for i in range(2)]
        pctr = {"f": 0, "b": 0}

        def nf():
            i = pctr["f"] % 6
            pctr["f"] += 1
            return psf[i], ("psf", i)

        def nb():
            i = pctr["b"] % 2
            pctr["b"] += 1
            return psb[i], ("psb", i)

        identb = sb("identb_s", [128, 128], BF16)
        identf = sb("identf_s", [128, 128])
        pm = sb("pm_s", [128, 128], BF16)
        S.dma("sp", identb[:], identb_d, writes=["identb"])
        S.dma("sp", identf[:], identf_d, writes=["identf"])
        S.dma("sp", pm[:], pm_d, writes=["pm"])

        CH = 128 * 4096
        NCH = WTOT // CH
        assert NCH * CH == WTOT
        with ExitStack() as wc:
            cin = [wc.enter_context(nc.sbuf_tensor("cin%d" % i, [128, 4096], F32)) for i in range(3)]
            cout = [wc.enter_context(nc.sbuf_tensor("cout%d" % i, [128, 4096], BF16)) for i in range(3)]
            for c in range(NCH):
                j = c % 3
                S.dma("sp", cin[j][:], wcat[c * CH:(c + 1) * CH].rearrange("(p x) -> p x", p=128), writes=[("cin", j)])
                en = ("act", "dve", "pool")[j]
                if en == "act":
                    S.op("act", lambda e, j=j: e.copy(out=cout[j][:], in_=cin[j][:]), reads=[("cin", j)], writes=[("cout", j)])
                else:
                    S.op(en, lambda e, j=j: e.tensor_copy(out=cout[j][:], in_=cin[j][:]), reads=[("cin", j)], writes=[("cout", j)])
                S.dma("act", wb[c * CH:(c + 1) * CH].rearrange("(p x) -> p x", p=128), cout[j][:],
                      reads=[("cout", j)], writes=[("wb", c)])
        S.barrier()
        if stop_after == "cast":
            S.finish()
            return nc
        wkeys = {}
        for name, kind, kc, nt in WSPEC:
            off, _, cw, _, per = lay[name]
            wkeys[name] = [("wb", c) for c in range(off // CH, (off + per * nt - 1) // CH + 1)]

        def wview(name, t):
            off, kc, cw, nt, per = lay[name]
            return wb[off + t * per: off + (t + 1) * per].rearrange("(p x) -> p x", p=128)

        B = {}
        wctr = {"fm": 0, "tm": 0}
        actr = [0]

        def alloc_common(ctx):
            actr[0] += 1

            def sbc(name, shape, dt=F32):
                B[name] = ctx.enter_context(nc.sbuf_tensor(name + "_%d" % actr[0], list(shape), dt))
            for i in range(3):
                sbc("wfm%d" % i, [128, 16 * 128], BF16)
            for i in range(2):
                sbc("wtm%d" % i, [128, 16 * 512], BF16)
            sbc("xbuf", [128, 4, D])
            sbc("xb16", [128, 2, D], BF16)
            sbc("xT", [128, 16, 512], BF16)
            sbc("gtab", [128, D])
            sbc("btab", [128, D])
            sbc("stats", [128, 4, 4, 6])
            sbc("mv", [128, 4, 2])
            sbc("rstd", [128, 4])
            sbc("nmr", [128, 4])

        def load_fm(name, t):
            off, kc, cw, nt, per = lay[name]
            i = wctr["fm"] % 3
            wctr["fm"] += 1
            wt = B["wfm%d" % i]
            S.dma("sp", wt[:, 0:kc * 128], wview(name, t), reads=wkeys[name], writes=[("wfm", i)])
            return wt[:, 0:kc * 128].rearrange("p (k c) -> p k c", c=128), ("wfm", i)

        def load_tm(name, t):
            off, kc, cw, nt, per = lay[name]
            i = wctr["tm"] % 2
            wctr["tm"] += 1
            wt = B["wtm%d" % i]
            S.dma("sp", wt[:, 0:kc * 512], wview(name, t), reads=wkeys[name], writes=[("wtm", i)])
            return wt[:, 0:kc * 512].rearrange("p (k c) -> p k c", c=512), ("wtm", i)

        B["par"] = 0

        def kx(s_):
            return ("xbuf", B["par"], s_)

        def kxT(s_, kq_):
            return ("xT", B["par"], s_, kq_)

        def kTall():
            return [kxT(s_, kq_) for s_ in range(4) for kq_ in range(2)]

        def ln_gen(which, tr_dst=True, out_tile=None, out_writes=None, load_tabs=True):
            xbuf, xb16, xT, gtab, btab = B["xbuf"], B["xb16"], B["xT"], B["gtab"], B["btab"]
            stats, mv, rstd, nmr = B["stats"], B["mv"], B["rstd"], B["nmr"]
            par = B["par"]
            xks = [("xbuf", par, s_) for s_ in range(4)]
            if load_tabs:
                S.dma("sp", gtab[:], lng_d[which], writes=["gtab"])
                S.dma("sp", btab[:], lnb_d[which], writes=["btab"])
            for s in range(4):
                for c in range(4):
                    S.op("dve", lambda e, s=s, c=c: e.bn_stats(out=stats[:, s, c, :], in_=xbuf[:, s, c * 512:(c + 1) * 512]),
                         reads=[xks[s]], writes=[("stats", s, c)])
                yield
            for s in range(4):
                S.op("dve", lambda e, s=s: e.bn_aggr(out=mv[:, s, :], in_=stats[:, s, :, :].rearrange("p a b -> p (a b)")),
                     reads=[("stats", s, c) for c in range(4)], writes=[("mv", s)])
            mvk = [("mv", s) for s in range(4)]
            S.op("dve", lambda e: e.tensor_scalar_add(out=rstd[:], in0=mv[:, :, 1], scalar1=LN_EPS), reads=mvk, writes=["rstd"])
            S.op("act", lambda e: e.sqrt(out=rstd[:], in_=rstd[:]), reads=["rstd"], writes=["rstd"])
            S.op("dve", lambda e: e.reciprocal(out=rstd[:], in_=rstd[:]), reads=["rstd"], writes=["rstd"])
            S.op("dve", lambda e: e.scalar_tensor_tensor(out=nmr[:], in0=mv[:, :, 0], scalar=-1.0, in1=rstd[:],
                                                         op0=ALU.mult, op1=ALU.mult), reads=mvk + ["rstd"], writes=["nmr"])
            yield
            for s in range(4):
                S.op("act", lambda e, s=s: e.activation(out=xbuf[:, s, :], in_=xbuf[:, s, :], func=AF.Identity,
                                                        bias=nmr[:, s:s + 1], scale=rstd[:, s:s + 1]),
                     reads=[xks[s], "rstd", "nmr"], writes=[xks[s]])
            yield
            for s in range(4):
                S.op("dve", lambda e, s=s: e.tensor_tensor(out=xbuf[:, s, :], in0=xbuf[:, s, :], in1=gtab[:], op=ALU.mult),
                     reads=[xks[s], "gtab"], writes=[xks[s]])
                yield
            ykeys = []
            for s in range(4):
                ydst = xbuf[:, s, :] if out_tile is None else out_tile[:, s, :]
                ykey = xks[s] if out_tile is None else ("lnout", s)
                ykeys.append(ykey)
                S.op("pool", lambda e, s=s, ydst=ydst: e.tensor_tensor(out=ydst, in0=xbuf[:, s, :], in1=btab[:], op=ALU.add),
                     reads=[xks[s], "btab"], writes=[ykey] + (out_writes[s] if out_writes else []))
            yield
            if tr_dst:
                for s in range(4):
                    sl = s % 2
                    S.op("act", lambda e, s=s, sl=sl: e.copy(out=xb16[:, sl, :], in_=xbuf[:, s, :]), reads=[ykeys[s]], writes=[("xb16", sl)])
                    for kq in range(2):
                        pt, pk = nb()
                        S.tr([(pt[:, j * 128:(j + 1) * 128], xb16[:, sl, (kq * 8 + j) * 128:(kq * 8 + j + 1) * 128])
                              for j in range(8)], identb[:], reads=[("xb16", sl), "identb"], writes=[pk])
                        src = pt[:, :].rearrange("p (k c) -> p k c", c=128)
                        dst = xT[:, kq * 8:(kq + 1) * 8, s * 128:(s + 1) * 128]
                        if kq == 0:
                            S.op("dve", lambda e, src=src, dst=dst: e.tensor_copy(out=dst, in_=src),
                                 reads=[pk], writes=[("xT", par, s, kq)])
                        else:
                            S.op("act", lambda e, src=src, dst=dst: e.copy(out=dst, in_=src),
                                 reads=[pk], writes=[("xT", par, s, kq)])
                    yield

        def layer_norm_tile(which, **kw):
            for _ in ln_gen(which, **kw):
                pass

        def load_x_tile(i):
            for s in range(4):
                S.dma("sp", B["xbuf"][:, s, :], x_in[i * 512 + s * 128: i * 512 + (s + 1) * 128, :],
                      writes=[kx(s)])

        with ExitStack() as p1:
            def sb1(name, shape, dt=F32):
                return p1.enter_context(nc.sbuf_tensor(name, list(shape), dt))
            alloc_common(p1)
            xbufs = [B["xbuf"], sb1("xbuf_b", [128, 4, D])]
            xTs = [B["xT"], sb1("xT_b", [128, 16, 512], BF16)]

            def set_par(par):
                B["par"] = par
                B["xbuf"] = xbufs[par]
                B["xT"] = xTs[par]
            cosb = [sb1("cosb0", [32, 512])] * 2
            sinb = [sb1("sinb0", [32, 512])] * 2
            tb = [sb1("tb%d" % i, [128, 512], BF16) for i in range(2)]
            ra = sb1("ra", [32, 512])
            rb = sb1("rb", [32, 512])
            qst = [sb1("qst%d" % i, [128, 512], BF16) for i in range(3)]
            vst = [sb1("vst%d" % i_, [128, 4, 512], BF16) for i_ in range(2)]
            ust = sb1("ust", [64, 64, 8, 16], BF16)
            S.dma("sp", B["gtab"][:], lng_d[0], writes=["gtab"])
            S.dma("sp", B["btab"][:], lnb_d[0], writes=["btab"])
            set_par(0)
            load_x_tile(0)
            layer_norm_tile(0, load_tabs=False)
            for i in range(NTT):
                cur = i % 2
                set_par(cur)
                xT = B["xT"]
                cb, sn = cosb[cur], sinb[cur]
                S.dma("sp", cb[:], ropec_d[:, i * 512:(i + 1) * 512], writes=[("cosb", 0)])
                S.dma("sp", sn[:], ropes_d[:, i * 512:(i + 1) * 512], writes=[("sinb", 0)])

                def rot_finish(m, ps, pk):
                    d = DIL[(m % 12) // 4]
                    ld = 512 // d
                    q = qst[m % 3]
                    qk_ = ("qst", m % 3)
                    t_ = tb[m % 2]
                    ps2, pk2 = nf()
                    S.mm(ps2[:, :], [(pm[:, :], t_[:])], reads=["pm", ("tb", m % 2)], writes=[pk2])
                    S.op("dve", lambda e: e.tensor_tensor(out=ra[:], in0=ps[0:32, :], in1=cb[:], op=ALU.mult),
                         reads=[pk, ("cosb", 0)], writes=["ra"])
                    S.op("dve", lambda e: e.tensor_tensor(out=rb[:], in0=ps2[0:32, :], in1=sn[:], op=ALU.mult),
                         reads=[pk2, ("sinb", 0)], writes=["rb"])

                    def vw_out(ap):
                        return ap if d == 1 else ap.rearrange("p (r l) -> p l r", r=d)

                    def vw_in(ap):
                        return ap if d == 1 else ap.rearrange("p (l r) -> p l r", r=d)
                    S.op("act", lambda e: e.copy(out=vw_out(q[:, :]), in_=vw_in(ps[:, :])), reads=[pk], writes=[qk_])
                    S.op("dve", lambda e: e.tensor_tensor(out=vw_out(q[0:32, :]), in0=vw_in(ra[:]), in1=vw_in(rb[:]), op=ALU.add),
                         reads=["ra", "rb"], writes=[qk_])
                    qdst = qk_scr[m].rearrange("p (r l) -> p r l", r=d)[:, :, i * ld:(i + 1) * ld]
                    qsrc = q[:, :].rearrange("p (r l) -> p r l", r=d)
                    RC = 8 if d == 16 else d
                    for r0 in range(0, d, RC):
                        S.dma("act", qdst[:, r0:r0 + RC, :], qsrc[:, r0:r0 + RC, :], reads=[qk_], writes=[("qk_scr", m, r0)])

                pend = None
                for m in range(24):
                    w, wk = load_fm("qk", m)
                    ps, pk = nf()
                    S.mm(ps[:, :], [(w[:, k, :], xT[:, k, :]) for k in range(16)], reads=[wk] + kTall(), writes=[pk])
                    S.op("act", lambda e, ps=ps, m=m: e.copy(out=tb[m % 2][:], in_=ps[:, :]), reads=[pk], writes=[("tb", m % 2)])
                    if pend is not None:
                        rot_finish(*pend)
                    pend = (m, ps, pk)
                    if i + 1 < NTT and m >= 1:
                        set_par(1 - cur)
                        if m == 1:
                            load_x_tile(i + 1)
                            lng_ = ln_gen(0, load_tabs=False)
                        next(lng_, None)
                        set_par(cur)
                rot_finish(*pend)
                if i + 1 < NTT:
                    set_par(1 - cur)
                    for _ in lng_:
                        pass
                    set_par(cur)
                for c in range(3):
                    w, wk = load_tm("v", c)
                    vs_ = vst[c % 2]
                    for s in range(4):
                        ps, pk = nf()
                        S.mm(ps[:, :], [(xT[:, k, s * 128:(s + 1) * 128], w[:, k, :]) for k in range(16)],
                             reads=[wk, kxT(s, 0), kxT(s, 1)], writes=[pk])
                        if s % 2 == 0:
                            S.op("act", lambda e, ps=ps, s=s: e.copy(out=vs_[:, s, :], in_=ps[:, :]),
                                 reads=[pk], writes=[("vst", c % 2, s)])
                        else:
                            S.op("dve", lambda e, ps=ps, s=s: e.tensor_copy(out=vs_[:, s, :], in_=ps[:, :]),
                                 reads=[pk], writes=[("vst", c % 2, s)])
                    S.dma("act", v_scr[i * 512:(i + 1) * 512, c * 512:(c + 1) * 512].rearrange("(s p) c -> p s c", p=128), vs_[:],
                          reads=[("vst", c % 2, s) for s in range(4)], writes=[("v_scr", c)])
                for c in range(2):
                    w, wk = load_tm("u", c)
                    for sg in range(8):
                        ps, pk = nf()
                        S.mm(ps[0:64, :], [(xT[:, k, :].rearrange("p (n s) -> p s n", s=8)[:, sg, :], w[:, k, :])
                                           for k in range(16)], reads=[wk] + kTall(), writes=[pk])
                        dstv = ust[:, c * 32:(c + 1) * 32, sg, :]
                        srcv = ps[0:64, :].rearrange("p (g c) -> p g c", c=16)
                        if sg % 2 == 0:
                            S.op("act", lambda e, dstv=dstv, srcv=srcv: e.copy(out=dstv, in_=srcv), reads=[pk], writes=[("ust", sg, c)])
                        else:
                            S.op("dve", lambda e, dstv=dstv, srcv=srcv: e.tensor_copy(out=dstv, in_=srcv), reads=[pk], writes=[("ust", sg, c)])
                S.dma("act", u_scr[i * 64:(i + 1) * 64], ust[:],
                      reads=[("ust", sg, c) for sg in range(8) for c in range(2)], writes=["u_scr"])
            set_par(0)

        S.barrier()
        if stop_after in ("pass1", "p1ln", "p1qk", "p1v"):
            S.finish()
            return nc

        with ExitStack() as pss:
            def sbs(name, shape, dt=F32):
                return pss.enter_context(nc.sbuf_tensor(name, list(shape), dt))
            WS8 = sbs("WS8", [128, 256, 64], BF16)
            WC8 = sbs("WC8", [128, 128, 128], BF16)
            D8 = sbs("D8", [128, 64, 128], BF16)
            A1x = sbs("A1x", [128, 2, 64])
            A2x = sbs("A2x", [128, 2, 64])
            PI_ = math.pi

            def tt(en, out, a, b, op, reads, writes):
                S.op(en, lambda e: e.tensor_tensor(out=out, in0=a, in1=b, op=op), reads=reads, writes=writes)

            with ExitStack() as pp:
                def sbp(name, shape, dt=F32):
                    return pp.enter_context(nc.sbuf_tensor(name, list(shape), dt))
                names = ["LRE", "LIM", "DT", "M", "TH", "MAG", "MAGI", "TS", "TC", "SN", "CS", "ARE", "AIM", "IRE", "IIM",
                         "ERE", "DEN", "T", "NRE", "NIM", "CR", "CI", "K1"]
                V = {n: sbp("v_" + n, [128, 128]) for n in names}
                KI = sbp("KI", [128, 128], mybir.dt.int32)
                maskf = sbp("maskf_s", [128, 128])
                maskb = sbp("maskb_s", [128, 128])
                S.dma("sp", maskf[:], maskf_d, writes=["maskf"])
                S.dma("sp", maskb[:], maskb_d, writes=["maskb"])
                S.dma("sp", V["LRE"][:], lamre_d, writes=["LRE"])
                S.dma("sp", V["LIM"][:], lamim_d, writes=["LIM"])
                S.dma("sp", V["DT"][:], logdt_d, writes=["DT"])

                def v(n):
                    return V[n][:]

                def ew(out, a, b, op, en="dve"):
                    tt(en, v(out), v(a), v(b), op, [a, b], [out])

                def ts(out, a, s1, s2, op0, op1=None):
                    if op1 is None:
                        S.op("dve", lambda e: e.tensor_single_scalar(out=v(out), in_=v(a), scalar=s1, op=op0), reads=[a], writes=[out])
                    else:
                        S.op("dve", lambda e: e.tensor_scalar(out=v(out), in0=v(a), scalar1=s1, scalar2=s2, op0=op0, op1=op1),
                             reads=[a], writes=[out])

                def actf(out, a, func, scale=1.0):
                    S.op("act", lambda e: e.activation(out=v(out), in_=v(a), func=func, scale=scale), reads=[a], writes=[out])

                actf("DT", "DT", AF.Exp)
                ts("LRE", "LRE", -1e-4, None, ALU.min)
                ew("M", "LRE", "DT", ALU.mult)
                ew("TH", "LIM", "DT", ALU.mult)
                actf("MAG", "M", AF.Exp)
                actf("MAGI", "M", AF.Exp, scale=-1.0)

                def reduce_angle(out, shift):
                    ts("K1", "TH", shift, 1.0 / TWO_PI, ALU.add, ALU.mult)
                    S.op("dve", lambda e: e.tensor_copy(out=KI[:], in_=v("K1")), reads=["K1"], writes=["KI"])
                    S.op("dve", lambda e: e.tensor_copy(out=v("K1"), in_=KI[:]), reads=["KI"], writes=["K1"])
                    ts("K1", "K1", -TWO_PI, None, ALU.mult)
                    ts(out, "TH", shift, None, ALU.add)
                    ew(out, out, "K1", ALU.add)
                    ts("K1", out, PI_, TWO_PI, ALU.is_gt, ALU.mult)
                    ew(out, out, "K1", ALU.subtract)
                    ts("K1", out, -PI_, TWO_PI, ALU.is_lt, ALU.mult)
                    ew(out, out, "K1", ALU.add)
                    ts(out, out, PI_, None, ALU.min)
                    ts(out, out, -PI_, None, ALU.max)

                reduce_angle("TS", 0.0)
                reduce_angle("TC", PI_ / 2.0)
                actf("SN", "TS", AF.Sin)
                actf("CS", "TC", AF.Sin)
                ew("ARE", "MAG", "CS", ALU.mult)
                ew("AIM", "MAG", "SN", ALU.mult)
                ew("IRE", "MAGI", "CS", ALU.mult)
                ew("IIM", "MAGI", "SN", ALU.mult)
                ts("IIM", "IIM", -1.0, None, ALU.mult)
                ts("ERE", "ARE", -1.0, None, ALU.add)
                ew("DEN", "LRE", "LRE", ALU.mult)
                ew("T", "LIM", "LIM", ALU.mult)
                ew("DEN", "DEN", "T", ALU.add)
                S.op("dve", lambda e: e.reciprocal(out=v("DEN"), in_=v("DEN")), reads=["DEN"], writes=["DEN"])
                ew("NRE", "ERE", "LRE", ALU.mult)
                ew("T", "AIM", "LIM", ALU.mult)
                ew("NRE", "NRE", "T", ALU.add)
                ew("NIM", "AIM", "LRE", ALU.mult)
                ew("T", "ERE", "LIM", ALU.mult)
                ew("NIM", "NIM", "T", ALU.subtract)
                ew("CR", "NRE", "DEN", ALU.mult)
                ew("CI", "NIM", "DEN", ALU.mult)
                PAr = sbp("PAr", [128, 9, 128])
                PAi = sbp("PAi", [128, 9, 128])
                PIr = sbp("PIr", [128, 9, 128])
                PIi = sbp("PIi", [128, 9, 128])
                for (Pr, Pi, nr, ni, tag) in ((PAr, PAi, "ARE", "AIM", "PA"), (PIr, PIi, "IRE", "IIM", "PI")):
                    S.op("dve", lambda e, Pr=Pr: e.memset(Pr[:, 0, :], 1.0), writes=[(tag, "r", 0)])
                    S.op("dve", lambda e, Pi=Pi: e.memset(Pi[:, 0, :], 0.0), writes=[(tag, "i", 0)])
                    for k in range(1, 9):
                        pr, pi_ = Pr[:, k - 1, :], Pi[:, k - 1, :]
                        kr, ki = (tag, "r", k - 1), (tag, "i", k - 1)
                        tt("dve", v("T"), pr, v(nr), ALU.mult, [kr, nr], ["T"])
                        tt("dve", v("K1"), pi_, v(ni), ALU.mult, [ki, ni], ["K1"])
                        tt("dve", Pr[:, k, :], v("T"), v("K1"), ALU.subtract, ["T", "K1"], [(tag, "r", k)])
                        tt("dve", v("T"), pr, v(ni), ALU.mult, [kr, ni], ["T"])
                        tt("dve", v("K1"), pi_, v(nr), ALU.mult, [ki, nr], ["K1"])
                        tt("dve", Pi[:, k, :], v("T"), v("K1"), ALU.add, ["T", "K1"], [(tag, "i", k)])
                PAk = [("PA", c, k) for c in "ri" for k in range(9)]
                PIk = [("PI", c, k) for c in "ri" for k in range(9)]
                for dr in range(2):
                    for (dst, src, sgn_lo, key) in ((A1x, PAr, 1.0, "A1x"), (A2x, PAi, -1.0, "A2x")):
                        for hf in range(2):
                            rows = slice(hf * 64, (hf + 1) * 64)
                            srcv = src[rows, 8, dr * 64:(dr + 1) * 64].rearrange("p (a two) -> p a two", two=2)[:, :, hf]
                            S.op("dve", lambda e, dst=dst, rows=rows, srcv=srcv, dr=dr, sgn_lo=sgn_lo: e.tensor_single_scalar(
                                out=dst[rows, dr, 0:32], in_=srcv, scalar=sgn_lo, op=ALU.mult), reads=PAk, writes=[(key, dr, hf, 0)])
                            S.op("dve", lambda e, dst=dst, rows=rows, srcv=srcv, dr=dr: e.tensor_copy(
                                out=dst[rows, dr, 32:64], in_=srcv), reads=PAk, writes=[(key, dr, hf, 1)])
                Br = sbp("Br", [128, 128, 16])
                Bi = sbp("Bi", [128, 128, 16])
                Cr = sbp("Cr", [128, 128, 16])
                Ci = sbp("Ci", [128, 128, 16])
                T1 = sbp("T1", [128, 16, 8, 16])
                T2 = sbp("T2", [128, 16, 8, 16])
                X1v = T1[:].rearrange("p a b c -> p (a b) c")
                X2v = T2[:].rearrange("p a b c -> p (a b) c")
                S.dma("sp", Br[:], bre_d.rearrange("p d g c -> p (d g) c"), writes=["Br"])
                S.dma("sp", Bi[:], bim_d.rearrange("p d g c -> p (d g) c"), writes=["Bi"])
                S.dma("sp", Cr[:], cre_d.rearrange("p d g c -> p (d g) c"), writes=["Cr"])
                S.dma("sp", Ci[:], cim_d.rearrange("p d g c -> p (d g) c"), writes=["Ci"])
                crb = v("CR").unsqueeze(2).to_broadcast([128, 128, 16])
                cib = v("CI").unsqueeze(2).to_broadcast([128, 128, 16])
                tt("dve", X1v, Br[:], crb, ALU.mult, ["Br", "CR"], ["T1"])
                tt("pool", X2v, Bi[:], cib, ALU.mult, ["Bi", "CI"], ["T2"])
                tt("dve", X1v, X1v, X2v, ALU.subtract, ["T1", "T2"], ["T1"])
                tt("pool", X2v, Br[:], cib, ALU.mult, ["Br", "CI"], ["T2"])
                tt("dve", Bi[:], Bi[:], crb, ALU.mult, ["Bi", "CR"], ["Bi"])
                tt("dve", Bi[:], Bi[:], X2v, ALU.add, ["Bi", "T2"], ["Bi"])
                S.op("dve", lambda e: e.tensor_copy(out=Br[:], in_=X1v), reads=["T1"], writes=["Br"])
                GQ = 16
                XSr = sbp("XSr", [128, GQ, 8, 16])
                XSi = sbp("XSi", [128, GQ, 8, 16])
                YDr = sbp("YDr", [128, GQ, 8, 16])
                YDi = sbp("YDi", [128, GQ, 8, 16])
                SELr = sbp("SELr", [128, GQ, 8])
                SELi = sbp("SELi", [128, GQ, 8])
                D8a = sbp("D8a", [128, GQ, 128])
                full = [128, GQ, 8, 16]

                def build_sel(Pr, Pi, pk, exps, cols):
                    for sgm in range(8):
                        S.op("dve", lambda e, sgm=sgm: e.tensor_copy(out=SELr[:, :, sgm], in_=Pr[:, exps[sgm], cols]), reads=pk, writes=[("SELr", sgm)])
                        S.op("pool", lambda e, sgm=sgm: e.tensor_copy(out=SELi[:, :, sgm], in_=Pi[:, exps[sgm], cols]), reads=pk, writes=[("SELi", sgm)])

                selk = [("SELr", q_) for q_ in range(8)] + [("SELi", q_) for q_ in range(8)]

                def cmul_tab(outr, outi, okr, oki, Mr, Mi, mkeys, cols):
                    sr = SELr[:, :, :].unsqueeze(3).to_broadcast(full)
                    si = SELi[:, :, :].unsqueeze(3).to_broadcast(full)
                    mr = Mr[:, cols, :].unsqueeze(2).to_broadcast(full)
                    mi = Mi[:, cols, :].unsqueeze(2).to_broadcast(full)
                    tt("dve", T1[:], sr, mr, ALU.mult, selk + mkeys, ["T1"])
                    tt("pool", T2[:], si, mi, ALU.mult, selk + mkeys, ["T2"])
                    tt("dve", outr[:], T1[:], T2[:], ALU.subtract, ["T1", "T2"], [okr])
                    tt("dve", T1[:], sr, mi, ALU.mult, selk + mkeys, ["T1"])
                    tt("pool", T2[:], si, mr, ALU.mult, selk + mkeys, ["T2"])
                    tt("dve", outi[:], T1[:], T2[:], ALU.add, ["T1", "T2"], [oki])

                for gq in range(64 // GQ):
                    for dr in range(2):
                        g0 = gq * GQ
                        cols = slice(dr * 64 + g0, dr * 64 + g0 + GQ)
                        build_sel(PAr, PAi, PAk, [7 - q_ for q_ in range(8)] if dr == 0 else list(range(8)), cols)
                        cmul_tab(XSr, XSi, "XSr", "XSi", Br, Bi, ["Br", "Bi"], cols)
                        for reim, (XS, xk) in enumerate(((XSr, "XSr"), (XSi, "XSi"))):
                            for q8 in range(GQ // 8):
                                ps, pk = nf()
                                S.tr([(ps[:, j * 64:(j + 1) * 64], XS[0:64, q8 * 8 + j, :, :].rearrange("p a b -> p (a b)")) for j in range(8)],
                                     identf[0:64, 0:64], reads=[xk, "identf"], writes=[pk])
                                i0 = (dr * 2 + reim) * 64 + g0 + q8 * 8
                                S.op("act", lambda e, ps=ps, i0=i0: e.copy(
                                    out=WS8[:, i0:i0 + 8, :], in_=ps[:, :].rearrange("p (j c) -> p j c", c=64)),
                                    reads=[pk], writes=[("WS8", i0)])
                        S.op("dve", lambda e: e.tensor_copy(out=XSr[64:128], in_=XSi[64:128]), reads=["XSi"], writes=["XSr"])
                        build_sel(PIr, PIi, PIk, [7 - q_ for q_ in range(8)] if dr == 0 else list(range(8)), cols)
                        cmul_tab(YDr, YDi, "YDr", "YDi", Cr, Ci, ["Cr", "Ci"], cols)
                        S.op("dve", lambda e: e.tensor_single_scalar(out=YDr[64:128], in_=YDi[64:128], scalar=-1.0, op=ALU.mult),
                             reads=["YDi"], writes=["YDr"])
                        for q4 in range(GQ // 4):
                            ps, pk = nf()
                            for j in range(4):
                                g = q4 * 4 + j
                                S.mm(ps[:, j * 128:(j + 1) * 128],
                                     [(XSr[:, g, :, :].rearrange("p a b -> p (a b)"), YDr[:, g, :, :].rearrange("p a b -> p (a b)"))],
                                     reads=["XSr", "YDr"], writes=[pk])
                            msk, mk = (maskf, "maskf") if dr == 0 else (maskb, "maskb")
                            mb_ = msk[:, :].unsqueeze(1).to_broadcast([128, 4, 128])
                            psv = ps[:, :].rearrange("p (j c) -> p j c", c=128)
                            dk = ("D8a", q4)
                            if dr == 0:
                                tt("dve", D8a[:, q4 * 4:(q4 + 1) * 4, :], psv, mb_, ALU.mult, [pk, mk], [dk])
                            else:
                                tt("dve", T1[:, 0:4, :, :].rearrange("p a b c -> p a (b c)"), psv, mb_, ALU.mult, [pk, mk], ["T1"])
                                tt("dve", D8[:, g0 + q4 * 4:g0 + (q4 + 1) * 4, :], D8a[:, q4 * 4:(q4 + 1) * 4, :],
                                   T1[:, 0:4, :, :].rearrange("p a b c -> p a (b c)"), ALU.add, [dk, "T1"], [("D8", g0 + q4 * 4)])
                        build_sel(PAr, PAi, PAk, [q_ + 1 for q_ in range(8)] if dr == 0 else [8 - q_ for q_ in range(8)], cols)
                        cmul_tab(YDr, YDi, "YDr", "YDi", Cr, Ci, ["Cr", "Ci"], cols)
                        for reim, (WT, wk_, sgn) in enumerate(((YDr, "YDr", 1.0), (YDi, "YDi", -1.0))):
                            i0 = (dr * 2 + reim) * 32 + g0 // 2
                            for hf in range(2):
                                rows = slice(hf * 64, (hf + 1) * 64)
                                srcv = WT[rows].rearrange("p (a two) t c -> p a two (t c)", two=2)[:, :, hf, :]
                                S.op("dve", lambda e, rows=rows, srcv=srcv, i0=i0, sgn=sgn: e.tensor_single_scalar(
                                    out=WC8[rows, i0:i0 + GQ // 2, :], in_=srcv, scalar=sgn, op=ALU.mult),
                                    reads=[wk_], writes=[("WC8", i0, hf)])
            S.barrier()
            if stop_after == "ssmprep":
                if debug:
                    dbg_ws = nc.dram_tensor("dbg_ws", [128, 256, 64], BF16, kind="ExternalOutput").ap()
                    dbg_wc = nc.dram_tensor("dbg_wc", [128, 128, 128], BF16, kind="ExternalOutput").ap()
                    dbg_d8 = nc.dram_tensor("dbg_d8", [128, 64, 128], BF16, kind="ExternalOutput").ap()
                    dbg_a = nc.dram_tensor("dbg_a", [128, 4, 64], F32, kind="ExternalOutput").ap()
                    S.dma("sp", dbg_ws, WS8[:], writes=["dbg1"])
                    S.dma("sp", dbg_wc, WC8[:], writes=["dbg2"])
                    S.dma("sp", dbg_d8, D8[:], writes=["dbg3"])
                    S.dma("sp", dbg_a[:, 0:2, :], A1x[:], writes=["dbg4"])
                    S.dma("sp", dbg_a[:, 2:4, :], A2x[:], writes=["dbg5"])
                S.finish()
                return nc
            Ut = sbs("Ut", [64, 64, 8, 16], BF16)
            UG = sbs("UG", [128, 64, 64], BF16)
            Sb = sbs("Sb", [128, 64, 64])
            HZ = sbs("HZ", [128, 65, 64])
            HF16 = sbs("HF16", [128, 64, 64], BF16)
            HB16 = sbs("HB16", [128, 64, 64], BF16)
            m1 = sbs("m1", [128, 64])
            m2 = sbs("m2", [128, 64])
            lkt = sbs("lkt", [128, NBLK + 1])
            DSK = sbs("DSK", [128, 64])
            BGL = sbs("BGL", [128, 8])
            yr = sbs("yr", [128, 8, 64])
            ysq = sbs("ysq", [128, 8, 64])
            ysg = sbs("ysg", [128, 8, 64])
            yact = [sbs("yact%d" % i, [128, 8, 64], BF16) for i in range(2)]
            Yt = sbs("Yt", [64, 8, 1024], BF16)
            yT = sbs("yT", [128, 8, 512], BF16)
            ssmst = sbs("ssmst", [128, 8, 512], BF16)
            wgl = [sbs("wgl%d" % i, [128, 8, 128], BF16) for i in range(2)]
            S.dma("sp", lkt[:], lk_d, writes=["lkt"])
            S.dma("sp", DSK[:], dsk_d, writes=["DSK"])
            S.dma("sp", BGL[:], bglu_d, writes=["BGL"])
            WS8k = [("WS8", i0) for i0 in range(0, 256, 8)]
            WC8k = [("WC8", i0, hf) for i0 in range(0, 128, 8) for hf in range(2)]
            D8k = [("D8", g) for g in range(0, 64, 4)]
            UGk = [("UG", q) for q in range(8)]

            def load_U(bk):
                S.dma("sp", Ut[:], u_scr[bk * 64:(bk + 1) * 64], reads=["u_scr"], writes=["Ut"])
                for g8 in range(8):
                    pt, pk = nb()
                    S.tr([(pt[:, j * 64:(j + 1) * 64], Ut[0:64, g8 * 8 + j, :, :].rearrange("p a b -> p (a b)")) for j in range(8)],
                         identb[0:64, 0:64], reads=["Ut", "identb"], writes=[pk])
                    src = pt[:, 0:512].rearrange("p (j n) -> p j n", n=64)
                    if g8 % 2 == 0:
                        S.op("act", lambda e, src=src, g8=g8: e.copy(out=UG[:, g8 * 8:(g8 + 1) * 8, :], in_=src), reads=[pk], writes=[("UG", g8)])
                    else:
                        S.op("dve", lambda e, src=src, g8=g8: e.tensor_copy(out=UG[:, g8 * 8:(g8 + 1) * 8, :], in_=src), reads=[pk], writes=[("UG", g8)])

            def compute_S(dr):
                for q8 in range(8):
                    ps, pk = nf()
                    for reim in range(2):
                        for j4 in range(4):
                            for g2 in range(2):
                                g = (q8 * 4 + j4) * 2 + g2
                                sl = reim * 4 + j4
                                S.mm(ps[g2 * 64:(g2 + 1) * 64, sl * 64:(sl + 1) * 64],
                                     [(WS8[:, (dr * 2 + reim) * 64 + g, :], UG[:, g, :])], reads=WS8k + UGk, writes=[pk])
                    for reim in range(2):
                        src = ps[:, reim * 256:(reim + 1) * 256].rearrange("p (j n) -> p n j", n=64)
                        dst = Sb[:, :, reim * 32 + q8 * 4: reim * 32 + q8 * 4 + 4]
                        if reim == 0:
                            S.op("act", lambda e, src=src, dst=dst: e.copy(out=dst, in_=src), reads=[pk], writes=[("Sb", q8, reim)])
                        else:
                            S.op("dve", lambda e, src=src, dst=dst: e.tensor_copy(out=dst, in_=src), reads=[pk], writes=[("Sb", q8, reim)])

            Sbk = [("Sb", q8, reim) for q8 in range(8) for reim in range(2)]

            def scan(dr):
                order = range(64) if dr == 0 else range(63, -1, -1)
                for jn in order:
                    pv, nw = (jn, jn + 1) if dr == 0 else (jn + 1, jn)
                    tt("dve", m1[:], A1x[:, dr, :], HZ[:, pv, 0:64], ALU.mult, [("HZ", pv)], ["m1"])
                    tt("dve", m2[:, 0:32], A2x[:, dr, 0:32], HZ[:, pv, 32:64], ALU.mult, [("HZ", pv)], ["m2a"])
                    tt("dve", m2[:, 32:64], A2x[:, dr, 32:64], HZ[:, pv, 0:32], ALU.mult, [("HZ", pv)], ["m2b"])
                    tt("dve", m1[:], m1[:], m2[:], ALU.add, ["m1", "m2a", "m2b"], ["m1"])
                    tt("dve", HZ[:, nw, 0:64], m1[:], Sb[:, jn, :], ALU.add, ["m1"] + Sbk, [("HZ", nw)])

            HZk = [("HZ", q) for q in range(65)]
            S.op("dve", lambda e: e.memset(HB16[:], 0.0), writes=["HB16"])
            S.dma("act", hb_scr[:, NSUP:NSUP + 1, :], HB16[:, 0:1, :], reads=["HB16"], writes=["hb_scr"])
            S.op("dve", lambda e: e.memset(HZ[:, 64, :], 0.0), writes=[("HZ", 64)])
            for bk in range(NBLK - 1, -1, -1):
                load_U(bk)
                compute_S(1)
                scan(1)
                S.op("dve", lambda e, bk=bk: e.tensor_scalar(out=HZ[:, 0, :], in0=HZ[:, 0, :], scalar1=lkt[:, bk:bk + 1], scalar2=None,
                                                            op0=ALU.mult), reads=[("HZ", 0), "lkt"], writes=[("HZ", 0)])
                S.op("act", lambda e: e.copy(out=HB16[:], in_=HZ[:, 0:64, 0:64]), reads=HZk, writes=["HB16"])
                S.dma("act", hb_scr[:, bk * 64:(bk + 1) * 64, :], HB16[:], reads=["HB16"], writes=["hb_scr"])
                S.op("dve", lambda e: e.tensor_copy(out=HZ[:, 64, :], in_=HZ[:, 0, :]), reads=[("HZ", 0)] + HZk, writes=[("HZ", 64)])
            S.op("dve", lambda e: e.memset(HZ[:, 0, :], 0.0), reads=HZk, writes=[("HZ", 0)])
            for bk in range(NBLK):
                load_U(bk)
                compute_S(0)
                scan(0)
                S.op("act", lambda e: e.copy(out=HF16[:], in_=HZ[:, 0:64, 0:64]), reads=HZk, writes=["HF16"])
                S.dma("sp", HB16[:], hb_scr[:, bk * 64 + 1: bk * 64 + 65, :], reads=["hb_scr"], writes=["HB16"])
                S.op("dve", lambda e, bk=bk: e.tensor_scalar(out=HZ[:, 0, :], in0=HZ[:, 64, :], scalar1=lkt[:, bk + 1:bk + 2], scalar2=None,
                                                            op0=ALU.mult), reads=HZk + ["lkt"], writes=[("HZ", 0)])
                def y_front(g8):
                    py, pyk = nf()
                    for j in range(8):
                        g = g8 * 8 + j
                        pair = g // 2
                        rows = slice(64 * (g % 2), 64 * (g % 2) + 64)
                        prs = [(D8[:, g, :], UG[:, g, :])]
                        for reim in range(2):
                            prs.append((WC8[rows, (0 * 2 + reim) * 32 + pair, :], HF16[rows, :, reim * 32 + pair]))
                        for reim in range(2):
                            prs.append((WC8[rows, (1 * 2 + reim) * 32 + pair, :], HB16[rows, :, reim * 32 + pair]))
                        S.mm(py[:, j * 64:(j + 1) * 64], prs, reads=D8k + WC8k + UGk + ["HF16", "HB16"], writes=[pyk])
                    ya = yact[g8 % 2]
                    yak = ("yact", g8 % 2)
                    pyv = py[:, :].rearrange("p (j n) -> p j n", n=64)
                    tt("dve", yr[:], UG[:, g8 * 8:(g8 + 1) * 8, :], DSK[:, g8 * 8:(g8 + 1) * 8].unsqueeze(2).to_broadcast([128, 8, 64]),
                       ALU.mult, UGk + ["DSK"], ["yr"])
                    tt("dve", yr[:], yr[:], pyv, ALU.add, ["yr", pyk], ["yr"])
                    S.op("act", lambda e: e.activation(out=ysq[:], in_=yr[:], func=AF.Square), reads=["yr"], writes=["ysq"])
                    S.op("dve", lambda e: e.tensor_scalar(out=ysq[:], in0=ysq[:], scalar1=0.044715, scalar2=1.0, op0=ALU.mult, op1=ALU.add),
                         reads=["ysq"], writes=["ysq"])
                    tt("dve", ysq[:], ysq[:], yr[:], ALU.mult, ["ysq", "yr"], ["ysq"])
                    S.op("act", lambda e: e.activation(out=ysg[:], in_=ysq[:], func=AF.Sigmoid, scale=1.5957691216057308),
                         reads=["ysq"], writes=["ysg"])
                    tt("dve", ya[:], yr[:], ysg[:], ALU.mult, ["yr", "ysg"], [yak])
                    return (g8, ya, yak)

                def y_back(g8, ya, yak):
                    pt, pk = nb()
                    S.tr([(pt[0:64, j * 128:(j + 1) * 128], ya[:, j, :]) for j in range(8)], identb[:], reads=[yak, "identb"], writes=[pk])
                    S.op("act", lambda e: e.copy(
                        out=Yt[:, :, g8 * 128:(g8 + 1) * 128].rearrange("n t (j c) -> n j t c", c=16),
                        in_=pt[0:64, :].rearrange("n (j t c) -> n j t c", j=8, t=8)), reads=[pk], writes=[("Yt", g8)])

                ypend = None
                for g8 in range(8):
                    ycur = y_front(g8)
                    if ypend is not None:
                        y_back(*ypend)
                    ypend = ycur
                y_back(*ypend)
                Ytk = [("Yt", q) for q in range(8)]
                for jt in range(8):
                    pt, pk = nb()
                    S.tr([(pt[:, t_ * 64:(t_ + 1) * 64], Yt[0:64, t_, jt * 128:(jt + 1) * 128]) for t_ in range(8)],
                         identb[0:64, 0:64], reads=Ytk + ["identb"], writes=[pk])
                    src = pt[:, 0:512].rearrange("p (t n) -> p t n", n=64)
                    dst = yT[:, jt, :].rearrange("p (n t) -> p t n", t=8)
                    if jt % 2 == 0:
                        S.op("act", lambda e, src=src, dst=dst: e.copy(out=dst, in_=src), reads=[pk], writes=[("yT", jt)])
                    else:
                        S.op("dve", lambda e, src=src, dst=dst: e.tensor_copy(out=dst, in_=src), reads=[pk], writes=[("yT", jt)])
                yTk = [("yT", q) for q in range(8)]
                for m in range(8):
                    wg = wgl[m % 2]
                    S.dma("sp", wg[:], wview("glu", m).rearrange("p (k c) -> p k c", c=128), reads=wkeys["glu"], writes=[("wgl", m % 2)])
                    ps, pk = nf()
                    S.mm(ps[:, :], [(wg[:, k, :], yT[:, k, :]) for k in range(8)], reads=[("wgl", m % 2)] + yTk, writes=[pk])
                    S.op("act", lambda e, ps=ps, m=m: e.activation(
                        out=ysq[:].rearrange("p a b -> p (a b)"), in_=ps[:, :], func=AF.Sigmoid, bias=BGL[:, m:m + 1]),
                         reads=[pk, "BGL"], writes=["ysq"])
                    tt("dve", ssmst[:, m, :], yT[:, m, :], ysq[:].rearrange("p a b -> p (a b)"), ALU.mult, [("yT", m), "ysq"], [("ssmst", m)])
                S.dma("act", ssm_scr[:, :, bk * 512:(bk + 1) * 512].rearrange("j p t -> p j t"), ssmst[:],
                      reads=[("ssmst", q) for q in range(8)], writes=["ssm_scr"])
        S.barrier()
        if stop_after == "ssm":
            S.finish()
            return nc

        with ExitStack() as pa:
            def sba(name, shape, dt=F32):
                return pa.enter_context(nc.sbuf_tensor(name, list(shape), dt))
            NB = NT // 128
            amask = sba("amask_s", [128, 3, 128], BF16)
            amaskl = sba("amaskl_s", [128, 3, 128], BF16)
            S.dma("sp", amask[:], amask_d, writes=["amask"])
            S.dma("sp", amaskl[:], amaskl_d, writes=["amaskl"])
            qT = [sba("aq%d" % i, [128, NT], BF16) for i in range(2)]
            kT = [sba("ak%d" % i, [128, NT], BF16) for i in range(2)]
            vA = [sba("av%d" % i, [128, NB, 129], BF16) for i in range(2)]
            ost = [sba("aost%d" % i, [128, NB, 129]) for i in range(2)]
            pexp = [sba("pexp%d" % i, [128, 3, 128], BF16) for i in range(3)]
            for i in range(2):
                S.op("dve", lambda e, i=i: e.memset(vA[i][:, :, 128:129], 1.0), writes=[("vAone", i)])
            for h in range(NH):
                d = DIL[h // 4]
                L = NT // d
                nbk = L // 128
                half = nbk // 2
                b = h % 2
                S.dma("sp", qT[b][:], qk_scr[h], reads=[("qk_scr", h, 0), ("qk_scr", h, 8)], writes=[("aq", b)])
                S.dma("sp", kT[b][:], qk_scr[12 + h], reads=[("qk_scr", 12 + h, 0), ("qk_scr", 12 + h, 8)], writes=[("ak", b)])
                vsrc = v_scr[:, h * 128:(h + 1) * 128].rearrange("(j m r) e -> r m j e", m=128, r=d)
                JC = 8
                for r in range(d):
                    for j0 in range(0, nbk, JC):
                        j1 = min(nbk, j0 + JC)
                        S.dma("sp", vA[b][:, r * nbk + j0:r * nbk + j1, 0:128], vsrc[r][:, j0:j1, :],
                              reads=[("v_scr", 0), ("v_scr", 1), ("v_scr", 2)], writes=[("vA", b, (r * nbk + j0) // JC)])
                def front(r, j):
                    blk = r * nbk + j
                    jjs = [jj for jj in (j - 1, j, j + 1) if 0 <= jj < nbk]
                    ps, pk = nf()
                    for jj in jjs:
                        dj = jj - j + 1
                        S.mm(ps[:, dj * 128:(dj + 1) * 128],
                             [(kT[b][:, (r * nbk + jj) * 128:(r * nbk + jj + 1) * 128], qT[b][:, blk * 128:(blk + 1) * 128])],
                             reads=[("ak", b), ("aq", b)], writes=[pk])
                    lo = jjs[0] - j + 1
                    hi = jjs[-1] - j + 2
                    pe_ = pexp[blk % 3]
                    pek = ("pexp", blk % 3)
                    S.op("act", lambda e: e.activation(
                        out=pe_[:, lo:hi, :], in_=ps[:, lo * 128:hi * 128].rearrange("p (a c) -> p a c", c=128),
                        func=AF.Exp, scale=SCALE), reads=[pk], writes=[pek])
                    crosses = [(j < half) != (jj < half) for jj in jjs]
                    if not any(crosses):
                        S.op("dve", lambda e: e.tensor_tensor(out=pe_[:, lo:hi, :], in0=pe_[:, lo:hi, :], in1=amask[:, lo:hi, :], op=ALU.mult),
                             reads=[pek, "amask"], writes=[pek])
                    else:
                        for jj, cross in zip(jjs, crosses):
                            dj = jj - j + 1
                            msk, mk = (amaskl, "amaskl") if cross else (amask, "amask")
                            S.op("dve", lambda e, dj=dj, msk=msk: e.tensor_tensor(
                                out=pe_[:, dj, :], in0=pe_[:, dj, :], in1=msk[:, dj, :], op=ALU.mult),
                                reads=[pek, mk], writes=[pek])
                    return (r, j, blk, jjs, pe_, pek)

                def back(r, j, blk, jjs, pe_, pek):
                    po, pok = nf()
                    S.mm(po[:, 0:129], [(pe_[:, jj - j + 1, :], vA[b][:, r * nbk + jj, :]) for jj in jjs],
                         reads=[pek, ("vAone", b)] + [("vA", b, (r * nbk + jj) // JC) for jj in jjs], writes=[pok])
                    ok_ = ("ost", b, blk % 4)
                    if blk % 2 == 0:
                        S.op("act", lambda e: e.copy(out=ost[b][:, blk, :], in_=po[:, 0:129]), reads=[pok], writes=[ok_])
                    else:
                        S.op("dve", lambda e: e.tensor_copy(out=ost[b][:, blk, :], in_=po[:, 0:129]), reads=[pok], writes=[ok_])

                pend_ = None
                for r in range(d):
                    for j in range(nbk):
                        cur_ = front(r, j)
                        if pend_ is not None:
                            back(*pend_)
                        pend_ = cur_
                back(*pend_)
                ndst = nd_scr[:, h, :].rearrange("(j m r) e -> r m j e", m=128, r=d)
                for r in range(d):
                    for j0 in range(0, nbk, JC):
                        j1 = min(nbk, j0 + JC)
                        S.dma("act", ndst[r][:, j0:j1, :], ost[b][:, r * nbk + j0:r * nbk + j1, :],
                              reads=[("ost", b, q_) for q_ in range(4)], writes=[("nd_scr", h, r, j0)])
        S.barrier()
        if stop_after == "attn":
            S.finish()
            return nc

        with ExitStack() as p2:
            def sb2(name, shape, dt=F32):
                return p2.enter_context(nc.sbuf_tensor(name, list(shape), dt))
            alloc_common(p2)
            xbuf, xT = B["xbuf"], B["xT"]
            big = sb2("big", [128, 64, 512], BF16)
            ndts = [sb2("ndt%d" % i_, [128, 12, 129]) for i_ in range(2)]
            rdens = [sb2("rden%d" % i_, [128, 4]) for i_ in range(2)]
            att16s = [sb2("att16_0", [128, 1536], BF16)] * 2
            sg = [sb2("sg%d" % i, [128, 512]) for i in range(2)]
            t1 = sb2("t1", [128, 512])
            pst = sb2("pst", [128, 256])
            p16 = sb2("p16", [128, 256], BF16)
            pT = sb2("pT", [128, 2, 512], BF16)
            sgc = [0]

            def nsg():
                i = sgc[0] % 2
                sgc[0] += 1
                return sg[i], ("sg", i)

            nd_keys = [k_ for k_ in S.R if isinstance(k_, tuple) and k_[0] == "nd_scr"]
            def nd_load(i, s):
                S.dma("sp", ndts[s % 2][:], nd_scr[i * 512 + s * 128: i * 512 + (s + 1) * 128], reads=nd_keys, writes=[("ndt", s % 2)])

            for i in range(NTT):
                load_x_tile(i)
                nd_load(i, 0)
                nd_load(i, 1)
                layer_norm_tile(0)
                for s in range(4):
                    ndt, rden, att16 = ndts[s % 2], rdens[s % 2], att16s[s % 2]
                    nk_, rk_, ak_ = ("ndt", s % 2), ("rden", s % 2), ("att16", 0)
                    S.op("dve", lambda e: e.tensor_tensor(out=rden[:], in0=ndt[:, 0:4, 128], in1=ndt[:, 4:8, 128], op=ALU.add),
                         reads=[nk_], writes=[rk_])
                    S.op("dve", lambda e: e.tensor_tensor(out=rden[:], in0=rden[:], in1=ndt[:, 8:12, 128], op=ALU.add),
                         reads=[nk_, rk_], writes=[rk_])
                    S.op("dve", lambda e: e.reciprocal(out=rden[:], in_=rden[:]), reads=[rk_], writes=[rk_])
                    for g3 in range(3):
                        S.op("dve", lambda e, g3=g3: e.tensor_tensor(
                            out=att16[:, g3 * 512:(g3 + 1) * 512].rearrange("p (j e) -> p j e", e=128),
                            in0=ndt[:, g3 * 4:(g3 + 1) * 4, 0:128],
                            in1=rden[:, :].unsqueeze(2).to_broadcast([128, 4, 128]), op=ALU.mult),
                            reads=[nk_, rk_], writes=[ak_])
                    if s + 2 < 4:
                        nd_load(i, s + 2)
                    for (k0, nk) in ((0, 8), (8, 4)):
                        pt, pk = nb()
                        S.tr([(pt[:, j * 128:(j + 1) * 128], att16[:, (k0 + j) * 128:(k0 + j + 1) * 128]) for j in range(nk)],
                             identb[:], reads=[ak_, "identb"], writes=[pk])
                        S.op("act", lambda e, pt=pt, k0=k0, nk=nk, s=s: e.copy(
                            out=big[:, k0:k0 + nk, s * 128:(s + 1) * 128],
                            in_=pt[:, 0:nk * 128].rearrange("p (k c) -> p k c", c=128)),
                            reads=[pk], writes=[("big", k) for k in range(k0, k0 + nk)])
                S.dma("sp", big[:, 12:20, :], ssm_scr[:, :, i * 512:(i + 1) * 512].rearrange("j p t -> p j t"),
                      reads=["ssm_scr"], writes=[("big", k) for k in range(12, 20)])
                att_keys = [("big", k) for k in range(12)]
                ssm_keys = [("big", k) for k in range(12, 20)]
                for m in range(16):
                    wga, k1 = load_fm("gate", m)
                    wao, k2 = load_fm("atto", m)
                    pg, pgk = nf()
                    S.mm(pg[:, :], [(wga[:, k, :], xT[:, k, :]) for k in range(16)], reads=[k1] + kTall(), writes=[pgk])
                    pa_, pak = nf()
                    S.mm(pa_[:, :], [(wao[:, k, :], big[:, k, :]) for k in range(12)], reads=[k2] + att_keys, writes=[pak])
                    s1, s1k = nsg()
                    S.op("act", lambda e, s1=s1, pg=pg: e.activation(out=s1[:], in_=pg[:, :], func=AF.Sigmoid), reads=[pgk], writes=[s1k])
                    S.op("dve", lambda e, s1=s1, pa_=pa_: e.tensor_tensor(out=t1[:], in0=s1[:], in1=pa_[:, :], op=ALU.mult),
                         reads=[s1k, pak], writes=["t1"])
                    wgs, k3 = load_fm("gate", 16 + m)
                    wso, k4 = load_fm("ssmo", m)
                    pg2, pg2k = nf()
                    S.mm(pg2[:, :], [(wgs[:, k, :], xT[:, k, :]) for k in range(16)], reads=[k3] + kTall(), writes=[pg2k])
                    pso, psok = nf()
                    S.mm(pso[:, :], [(wso[:, k, :], big[:, 12 + k, :]) for k in range(8)], reads=[k4] + ssm_keys, writes=[psok])
                    s2, s2k = nsg()
                    S.op("act", lambda e, s2=s2, pg2=pg2: e.activation(out=s2[:], in_=pg2[:, :], func=AF.Sigmoid), reads=[pg2k], writes=[s2k])
                    S.op("dve", lambda e, s2=s2, pso=pso: e.tensor_tensor(out=s2[:], in0=s2[:], in1=pso[:, :], op=ALU.mult),
                         reads=[s2k, psok], writes=[s2k])
                    S.op("dve", lambda e, m=m, s2=s2: e.tensor_tensor(out=big[:, 20 + m, :], in0=t1[:], in1=s2[:], op=ALU.add),
                         reads=["t1", s2k], writes=[("big", 20 + m)])
                mg_keys = [("big", 20 + k) for k in range(16)]
                for c in range(4):
                    w, wk = load_tm("out", c)
                    for s in range(4):
                        ps, pk = nf()
                        S.mm(ps[:, :], [(big[:, 20 + k, s * 128:(s + 1) * 128], w[:, k, :]) for k in range(16)],
                             reads=[wk] + mg_keys, writes=[pk])
                        S.op("dve", lambda e, ps=ps, s=s, c=c: e.scalar_tensor_tensor(
                            out=xbuf[:, s, c * 512:(c + 1) * 512], in0=xbuf[:, s, c * 512:(c + 1) * 512], scalar=ALPHA,
                            in1=ps[:, :], op0=ALU.mult, op1=ALU.add), reads=[pk, kx(s)], writes=[kx(s)])
                layer_norm_tile(1)
                for m in range(64):
                    w, wk = load_fm("up", m)
                    ps, pk = nf()
                    S.mm(ps[:, :], [(w[:, k, :], xT[:, k, :]) for k in range(16)], reads=[wk] + kTall(), writes=[pk])
                    s1, s1k = nsg()
                    S.op("act", lambda e, s1=s1, ps=ps: e.activation(out=s1[:], in_=ps[:, :], func=AF.Relu), reads=[pk], writes=[s1k])
                    S.op("dve", lambda e, s1=s1, m=m: e.tensor_tensor(out=big[:, m, :], in0=s1[:], in1=s1[:], op=ALU.mult),
                         reads=[s1k], writes=[("big", m)])
                h_keys = [("big", k) for k in range(64)]
                for s in range(4):
                    S.dma("sp", pst[:], p_in[i * 512 + s * 128: i * 512 + (s + 1) * 128, :], writes=["pst"])
                    S.op("act", lambda e, s=s: e.copy(out=p16[:], in_=pst[:]), reads=["pst"], writes=["p16"])
                    pt, pk = nb()
                    S.tr([(pt[:, j * 128:(j + 1) * 128], p16[:, j * 128:(j + 1) * 128]) for j in range(2)],
                         identb[:], reads=["p16", "identb"], writes=[pk])
                    S.op("dve", lambda e, pt=pt, s=s: e.tensor_copy(
                        out=pT[:, :, s * 128:(s + 1) * 128], in_=pt[:, 0:256].rearrange("p (k c) -> p k c", c=128)),
                        reads=[pk], writes=[("pT", s)])
                for c in range(4):
                    banks = [nf() for _ in range(4)]
                    for kg in range(4):
                        w, wk = load_tm("down", c * 4 + kg)
                        for s in range(4):
                            S.mm(banks[s][0][:, :], [(big[:, kg * 16 + k, s * 128:(s + 1) * 128], w[:, k, :]) for k in range(16)],
                                 reads=[wk] + h_keys[kg * 16:(kg + 1) * 16], writes=[banks[s][1]], start=(kg == 0), stop=(kg == 3))
                    for s in range(4):
                        S.op("dve", lambda e, s=s, c=c, ps=banks[s][0]: e.scalar_tensor_tensor(
                            out=xbuf[:, s, c * 512:(c + 1) * 512], in0=xbuf[:, s, c * 512:(c + 1) * 512], scalar=ALPHA,
                            in1=ps[:, :], op0=ALU.mult, op1=ALU.add), reads=[banks[s][1], kx(s)], writes=[kx(s)])
                    wpg, k1 = load_tm("pg", c)
                    wpp, k2 = load_tm("pp", c)
                    for s in range(4):
                        pg, pgk = nf()
                        S.mm(pg[:, :], [(xT[:, k, s * 128:(s + 1) * 128], wpg[:, k, :]) for k in range(16)],
                             reads=[k1, kxT(s, 0), kxT(s, 1)], writes=[pgk])
                        pp_, ppk = nf()
                        S.mm(pp_[:, :], [(pT[:, k, s * 128:(s + 1) * 128], wpp[:, k, :]) for k in range(2)],
                             reads=[k2, ("pT", s)], writes=[ppk])
                        s1, s1k = nsg()
                        S.op("act", lambda e, s1=s1, pg=pg: e.activation(out=s1[:], in_=pg[:, :], func=AF.Sigmoid), reads=[pgk], writes=[s1k])
                        S.op("dve", lambda e, s1=s1, pp_=pp_: e.tensor_tensor(out=t1[:], in0=s1[:], in1=pp_[:, :], op=ALU.mult),
                             reads=[s1k, ppk], writes=["t1"])
                        S.op("dve", lambda e, s=s, c=c: e.tensor_tensor(
                            out=xbuf[:, s, c * 512:(c + 1) * 512], in0=xbuf[:, s, c * 512:(c + 1) * 512], in1=t1[:], op=ALU.add),
                            reads=["t1", kx(s)], writes=[kx(s)])
                bigf = big[:].rearrange("p a b -> p (a b)").bitcast(F32)[:, 0:4 * D].rearrange("p (s d) -> p s d", d=D)
                layer_norm_tile(2, tr_dst=False, out_tile=bigf,
                                out_writes=[[("big", 8 * s_ + q_) for q_ in range(8)] for s_ in range(4)])
                for s in range(4):
                    S.dma("sp", y_out[i * 512 + s * 128: i * 512 + (s + 1) * 128, :], bigf[:, s, :],
                          reads=[("big", 8 * s + q_) for q_ in range(8)], writes=[("y", s)])
        S.finish()
    return nc


def _shared_maps(NT, ln_emb_g, ln_emb_b, w_in, ssm_lam_re, ssm_lam_im, ssm_log_dt, ssm_b_re, ssm_b_im,
                 ssm_c_re, ssm_c_im, ssm_d, w_glu, b_glu, w_att_o, w_ssm_o, w_out, ln1_g, ln1_b,
                 w_up, w_down, w_ple_gate, w_ple_proj, ln2_g, ln2_b):
    f = np.float32
    m = dict(_consts(NT))
    m["wcat"] = _host_weights(np.asarray(w_in[0], f), np.asarray(w_att_o[0], f), np.asarray(w_ssm_o[0], f),
                              np.asarray(w_out[0], f), np.asarray(w_up[0], f), np.asarray(w_down[0], f),
                              np.asarray(w_ple_gate[0], f), np.asarray(w_ple_proj[0], f), np.asarray(w_glu[0], f))
    g = np.stack([np.asarray(ln_emb_g, f), np.asarray(ln1_g[0], f), np.asarray(ln2_g[0], f)])
    b = np.stack([np.asarray(ln_emb_b, f), np.asarray(ln1_b[0], f), np.asarray(ln2_b[0], f)])
    m["lng"] = np.ascontiguousarray(np.broadcast_to(g[:, None, :], (3, 128, D)))
    m["lnb"] = np.ascontiguousarray(np.broadcast_to(b[:, None, :], (3, 128, D)))

    def dup_p(a):
        t = np.asarray(a, f).transpose(2, 0, 1).reshape(64, 128)
        return np.ascontiguousarray(np.concatenate([t, t], 0))
    m["lamre"] = dup_p(ssm_lam_re[0])
    m["lamim"] = dup_p(ssm_lam_im[0])
    m["logdt"] = np.ascontiguousarray(np.broadcast_to(np.asarray(ssm_log_dt[0], f).reshape(1, 128), (128, 128)))

    def dup_b(a):
        t = np.asarray(a, f).transpose(2, 0, 1, 3)
        return np.ascontiguousarray(np.concatenate([t, t], 0))

    def dup_c(a):
        t = np.asarray(a, f).transpose(3, 0, 1, 2)
        return np.ascontiguousarray(np.concatenate([t, t], 0))
    m["bre"] = dup_b(ssm_b_re[0])
    m["bim"] = dup_b(ssm_b_im[0])
    m["cre"] = dup_c(ssm_c_re[0])
    m["cim"] = dup_c(ssm_c_im[0])
    dsk = np.asarray(ssm_d[0], f).reshape(64, 16)
    m["dsk"] = np.ascontiguousarray(np.broadcast_to(dsk.T[None, :, :], (8, 16, 64)).reshape(128, 64))
    m["bglu"] = np.ascontiguousarray(np.asarray(b_glu[0], f).reshape(8, 128).T)
    return m


def _core_map(shared, NT, x, p, pos, link):
    m = dict(shared)
    NBLK = NT // 512
    m["x"] = np.ascontiguousarray(x, np.float32)
    m["pp_in"] = np.ascontiguousarray(p, np.float32)
    c, s = _rope(pos.astype(np.float32))
    m["ropec"] = np.ascontiguousarray(c)
    m["ropes"] = np.ascontiguousarray(s)
    lk = np.ones((128, NBLK + 1), np.float32)
    lk[:, NBLK // 2] = link
    m["lk"] = lk
    am = shared["amask"].astype(np.float32)
    m["amaskl"] = (am if link else np.zeros_like(am)).astype(ml_dtypes.bfloat16)
    return m


_NC_CACHE = {}


def kernel(x_prompt, x_sample, p_prompt, p_sample, ln_emb_g, ln_emb_b, w_in, ssm_lam_re, ssm_lam_im,
           ssm_log_dt, ssm_b_re, ssm_b_im, ssm_c_re, ssm_c_im, ssm_d, w_glu, b_glu, w_att_o, w_ssm_o,
           w_out, ln1_g, ln1_b, w_up, w_down, w_ple_gate, w_ple_proj, ln2_g, ln2_b):
    NT = 8192
    x_prompt = np.asarray(x_prompt, np.float32)
    x_sample = np.asarray(x_sample, np.float32)
    p_prompt = np.asarray(p_prompt, np.float32)
    p_sample = np.asarray(p_sample, np.float32)
    shared = _shared_maps(NT, ln_emb_g, ln_emb_b, w_in, ssm_lam_re, ssm_lam_im, ssm_log_dt, ssm_b_re, ssm_b_im,
                          ssm_c_re, ssm_c_im, ssm_d, w_glu, b_glu, w_att_o, w_ssm_o, w_out, ln1_g, ln1_b,
                          w_up, w_down, w_ple_gate, w_ple_proj, ln2_g, ln2_b)
    pos2 = np.concatenate([np.arange(4096), np.arange(4096)])
    pos1 = np.arange(8192)
    maps = []
    for c in range(4):
        maps.append(_core_map(shared, NT, x_prompt[2 * c:2 * c + 2].reshape(NT, D),
                              p_prompt[0, 2 * c:2 * c + 2].reshape(NT, 256), pos2, 0.0))
    for c in range(4):
        maps.append(_core_map(shared, NT, x_sample[c], p_sample[0, c], pos1, 1.0))
    if "nc" not in _NC_CACHE:
        _NC_CACHE["nc"] = build(NT)
    res = run_bass_kernel_spmd(_NC_CACHE["nc"], maps, core_ids=list(range(8)))
    ys = [np.asarray(r["y"], np.float32) for r in res.results]
    y_prompt = np.stack([ys[c].reshape(2, 4096, D) for c in range(4)]).reshape(8, 4096, D)
    y_sample = np.stack(ys[4:8]).reshape(4, 8192, D)
    return (y_prompt, y_sample)
```

```python
import math
from contextlib import ExitStack

import ml_dtypes
import numpy as np

import concourse.bass as bass
import concourse.mybir as mybir
from concourse.bass_utils import run_bass_kernel_spmd

F32 = mybir.dt.float32
BF16 = mybir.dt.bfloat16
AF = mybir.ActivationFunctionType
ALU = mybir.AluOpType
AX = mybir.AxisListType

D = 2048
DFF = 8192
ALPHA = 2.0 ** 0.25
LN_EPS = 1e-5
NH = 12
DIL = (1, 4, 16)
SCALE = 1.0 / math.sqrt(128.0)
TWO_PI = 2.0 * math.pi


class Sched:
    EPOCH = 30000

    def __init__(self, nc, st, nds=24):
        self.nc = nc
        self.st = st
        self.E = {}
        for nm, e in (("pe", nc.tensor), ("act", nc.scalar), ("dve", nc.vector),
                      ("pool", nc.gpsimd), ("sp", nc.sync)):
            self.E[nm] = dict(e=e, sems=[st.enter_context(nc.semaphore("s_%s0" % nm))], n=0, waited={})
        self.ds = [st.enter_context(nc.semaphore("d%d" % i)) for i in range(nds)]
        self.dn = [0] * nds
        self.rr = 0
        self.rr2 = 0
        self.R = {}

    def _res(self, k):
        r = self.R.get(k)
        if r is None:
            r = [{}, {}]
            self.R[k] = r
        return r

    def _wait(self, en, key, val):
        E = self.E[en]
        if key[0] == "e" and key[1] == en and en == "pe":
            return
        if E["waited"].get(key, 0) >= val:
            return
        if key[0] == "e":
            sem = self.E[key[1]]["sems"][key[2]]
        else:
            sem = self.ds[key[1]]
        E["e"].wait_ge(sem, val)
        E["waited"][key] = val

    def _deps(self, en, reads, writes):
        deps = {}

        def add(k, v):
            if deps.get(k, 0) < v:
                deps[k] = v
        for r in reads:
            for k, v in self._res(r)[0].items():
                add(k, v)
        for w in writes:
            W, Rd = self._res(w)
            for k, v in W.items():
                add(k, v)
            for k, v in Rd.items():
                add(k, v)
        for k, v in deps.items():
            self._wait(en, k, v)

    def _reg(self, key, val, reads, writes):
        for w in writes:
            self.R[w] = [{key: val}, {}]
        for r in reads:
            rd = self._res(r)[1]
            if rd.get(key, 0) < val:
                rd[key] = val

    def _inc(self, en, ins):
        E = self.E[en]
        E["n"] += 1
        ep = (E["n"] - 1) // self.EPOCH
        if ep >= len(E["sems"]):
            E["sems"].append(self.st.enter_context(self.nc.semaphore("s_%s%d" % (en, ep))))
        val = E["n"] - ep * self.EPOCH
        ins.then_inc(E["sems"][ep], 1)
        return ("e", en, ep), val

    def op(self, en, fn, reads=(), writes=()):
        writes = list(writes) + [r for r in reads if isinstance(r, tuple) and r[0] in ("psf", "psb")]
        self._deps(en, reads, writes)
        ins = fn(self.E[en]["e"])
        key, val = self._inc(en, ins)
        self._reg(key, val, reads, writes)

    def mm(self, out, pairs, reads, writes, start=True, stop=True):
        self._deps("pe", reads, writes)
        pe = self.E["pe"]["e"]
        n = len(pairs)
        ins = None
        for i, (l, r) in enumerate(pairs):
            ins = pe.matmul(out, lhsT=l, rhs=r, start=(start and i == 0), stop=(stop and i == n - 1))
        key, val = self._inc("pe", ins)
        self._reg(key, val, reads, writes)

    def tr(self, outs_ins, ident, reads, writes):
        self._deps("pe", reads, writes)
        pe = self.E["pe"]["e"]
        ins = None
        for o, i in outs_ins:
            ins = pe.transpose(o, i, ident)
        key, val = self._inc("pe", ins)
        self._reg(key, val, reads, writes)

    def dma(self, en, out, in_, reads=(), writes=(), **kw):
        self._deps(en, reads, writes)
        if en == "sp":
            idx = self.rr
            self.rr = (self.rr + 1) % 16
        else:
            idx = 16 + self.rr2
            self.rr2 = (self.rr2 + 1) % (len(self.ds) - 16)
        if self.dn[idx] > 0:
            self._wait(en, ("d", idx), self.dn[idx])
        self.E[en]["e"].dma_start(out=out, in_=in_, **kw).then_inc(self.ds[idx], 16)
        self.dn[idx] += 16
        self._reg(("d", idx), self.dn[idx], reads, writes)

    def barrier(self):
        mx = {}
        for k, r in self.R.items():
            for d in r:
                for kk, v in d.items():
                    if mx.get(kk, 0) < v:
                        mx[kk] = v
        for en in self.E:
            for kk, v in mx.items():
                self._wait(en, kk, v)

    def finish(self):
        for k, r in list(self.R.items()):
            for kk, v in list(r[0].items()) + list(r[1].items()):
                self._wait("sp", kk, v)


def _tile_fm(w):
    K, N = w.shape
    return np.ascontiguousarray(w.reshape(K // 128, 128, N // 128, 128).transpose(2, 1, 0, 3)).reshape(-1)


def _tile_tm(w, cw=512):
    K, N = w.shape
    return np.ascontiguousarray(w.reshape(K // 128, 128, N // cw, cw).transpose(2, 1, 0, 3)).reshape(-1)


def _tile_tm_kg(w, kg=4, cw=512):
    K, N = w.shape
    kc = K // 128 // kg
    return np.ascontiguousarray(w.reshape(kg, kc, 128, N // cw, cw).transpose(3, 0, 2, 1, 4)).reshape(-1)


WSPEC = [
    ("qk", "fm", 16, 24), ("v", "tm", 16, 3), ("u", "tm", 16, 2), ("gate", "fm", 16, 32),
    ("atto", "fm", 12, 16), ("ssmo", "fm", 8, 16), ("out", "tm", 16, 4), ("up", "fm", 16, 64),
    ("down", "tmkg", 16, 16), ("pg", "tm", 16, 4), ("pp", "tm", 2, 4), ("glu", "fm", 8, 8),
]


def _wlayout():
    off = 0
    lay = {}
    for name, kind, kc, nt in WSPEC:
        cw = 128 if kind == "fm" else 512
        per = 128 * kc * cw
        lay[name] = (off, kc, cw, nt, per)
        off += per * nt
    return lay, off


def _host_weights(w_in, w_att_o, w_ssm_o, w_out, w_up, w_down, w_ple_gate, w_ple_proj, w_glu):
    parts = [
        _tile_fm(w_in[:, 0:3072]), _tile_tm(w_in[:, 3072:4608]), _tile_tm(w_in[:, 4608:5632]),
        _tile_fm(w_in[:, 5632:9728]), _tile_fm(w_att_o), _tile_fm(w_ssm_o), _tile_tm(w_out),
        _tile_fm(w_up), _tile_tm_kg(w_down), _tile_tm(w_ple_gate), _tile_tm(w_ple_proj), _tile_fm(w_glu),
    ]
    return np.concatenate(parts)


def _consts(NT):
    c = {}
    c["identb"] = np.eye(128, dtype=np.float32).astype(ml_dtypes.bfloat16)
    c["identf"] = np.eye(128, dtype=np.float32)
    pm = np.zeros((128, 128), np.float32)
    for j in range(16):
        pm[16 + j, j] = -1.0
        pm[j, 16 + j] = 1.0
    c["pm"] = pm.astype(ml_dtypes.bfloat16)
    k = np.arange(128)[:, None]
    q = np.arange(128)[None, :]
    am = np.zeros((128, 3, 128), np.float32)
    am[:, 0, :] = (k - q >= 64)
    am[:, 1, :] = (np.abs(k - q) <= 64)
    am[:, 2, :] = (q - k >= 64)
    c["amask"] = am.astype(ml_dtypes.bfloat16)
    s = np.arange(128) // 16
    mf = (s[None, :] >= s[:, None]).astype(np.float32)
    mb = (s[:, None] >= s[None, :]).astype(np.float32)
    c["maskf"] = mf
    c["maskb"] = mb
    return c


def _rope(pos):
    inv = (500000.0 ** (-np.arange(0, 32, 2, dtype=np.float32) / np.float32(32))).astype(np.float32)
    ang = (pos[None, :].astype(np.float32) * inv[:, None]).astype(np.float32)
    cos = np.cos(ang.astype(np.float64)).astype(np.float32)
    sin = np.sin(ang.astype(np.float64)).astype(np.float32)
    return np.concatenate([cos, cos], 0), np.concatenate([sin, sin], 0)


def build(NT, debug=False, stop_after=None):
    nc = bass.Bass("TRN2", target_bir_lowering=False)
    NTT = NT // 512
    NSUP = NT // 8
    NBLK = NSUP // 64
    lay, WTOT = _wlayout()

    def din(name, shape, dt=F32):
        return nc.dram_tensor(name, list(shape), dt, kind="ExternalInput").ap()

    def dscr(name, shape, dt):
        return nc.dram_tensor(name, list(shape), dt, kind=("ExternalOutput" if debug else "Internal")).ap()

    x_in = din("x", [NT, D])
    p_in = din("pp_in", [NT, 256])
    wcat = din("wcat", [WTOT])
    identb_d = din("identb", [128, 128], BF16)
    identf_d = din("identf", [128, 128])
    pm_d = din("pm", [128, 128], BF16)
    amask_d = din("amask", [128, 3, 128], BF16)
    amaskl_d = din("amaskl", [128, 3, 128], BF16)
    maskf_d = din("maskf", [128, 128])
    maskb_d = din("maskb", [128, 128])
    ropec_d = din("ropec", [32, NT])
    ropes_d = din("ropes", [32, NT])
    lk_d = din("lk", [128, NBLK + 1])
    lng_d = din("lng", [3, 128, D])
    lnb_d = din("lnb", [3, 128, D])
    lamre_d = din("lamre", [128, 128])
    lamim_d = din("lamim", [128, 128])
    logdt_d = din("logdt", [128, 128])
    bre_d = din("bre", [128, 2, 64, 16])
    bim_d = din("bim", [128, 2, 64, 16])
    cre_d = din("cre", [128, 2, 64, 16])
    cim_d = din("cim", [128, 2, 64, 16])
    dsk_d = din("dsk", [128, 64])
    bglu_d = din("bglu", [128, 8])
    y_out = nc.dram_tensor("y", [NT, D], F32, kind="ExternalOutput").ap()

    wb = dscr("wb", [WTOT], BF16)
    qk_scr = dscr("qk_scr", [24, 128, NT], BF16)
    v_scr = dscr("v_scr", [NT, 1536], BF16)
    u_scr = dscr("u_scr", [NSUP, 64, 8, 16], BF16)
    hb_scr = dscr("hb_scr", [128, NSUP + 1, 64], BF16)
    ssm_scr = dscr("ssm_scr", [8, 128, NT], BF16)
    nd_scr = dscr("nd_scr", [NT, NH, 129], F32)

    st = ExitStack()
    with st:
        S = Sched(nc, st)

        def sb(name, shape, dt=F32):
            return st.enter_context(nc.sbuf_tensor(name, list(shape), dt))

        psf = [st.enter_context(nc.psum_tensor("psf%d" % i, [128, 512], F32)) for i in range(6)]
        psb = [st.enter_context(nc.psum_tensor("psb%d" % i, [128, 1024], BF16)) for i in range(2)]
        pctr = {"f": 0, "b": 0}

        def nf():
            i = pctr["f"] % 6
            pctr["f"] += 1
            return psf[i], ("psf", i)

        def nb():
            i = pctr["b"] % 2
            pctr["b"] += 1
            return psb[i], ("psb", i)

        identb = sb("identb_s", [128, 128], BF16)
        identf = sb("identf_s", [128, 128])
        pm = sb("pm_s", [128, 128], BF16)
        S.dma("sp", identb[:], identb_d, writes=["identb"])
        S.dma("sp", identf[:], identf_d, writes=["identf"])
        S.dma("sp", pm[:], pm_d, writes=["pm"])

        CH = 128 * 4096
        NCH = WTOT // CH
        assert NCH * CH == WTOT
        with ExitStack() as wc:
            cin = [wc.enter_context(nc.sbuf_tensor("cin%d" % i, [128, 4096], F32)) for i in range(3)]
            cout = [wc.enter_context(nc.sbuf_tensor("cout%d" % i, [128, 4096], BF16)) for i in range(3)]
            for c in range(NCH):
                j = c % 3
                S.dma("sp", cin[j][:], wcat[c * CH:(c + 1) * CH].rearrange("(p x) -> p x", p=128), writes=[("cin", j)])
                en = ("act", "dve", "pool")[j]
                if en == "act":
                    S.op("act", lambda e, j=j: e.copy(out=cout[j][:], in_=cin[j][:]), reads=[("cin", j)], writes=[("cout", j)])
                else:
                    S.op(en, lambda e, j=j: e.tensor_copy(out=cout[j][:], in_=cin[j][:]), reads=[("cin", j)], writes=[("cout", j)])
                S.dma("act", wb[c * CH:(c + 1) * CH].rearrange("(p x) -> p x", p=128), cout[j][:],
                      reads=[("cout", j)], writes=[("wb", c)])
        S.barrier()
        if stop_after == "cast":
            S.finish()
            return nc
        wkeys = {}
        for name, kind, kc, nt in WSPEC:
            off, _, cw, _, per = lay[name]
            wkeys[name] = [("wb", c) for c in range(off // CH, (off + per * nt - 1) // CH + 1)]

        def wview(name, t):
            off, kc, cw, nt, per = lay[name]
            return wb[off + t * per: off + (t + 1) * per].rearrange("(p x) -> p x", p=128)

        B = {}
        wctr = {"fm": 0, "tm": 0}
        actr = [0]

        def alloc_common(ctx):
            actr[0] += 1

            def sbc(name, shape, dt=F32):
                B[name] = ctx.enter_context(nc.sbuf_tensor(name + "_%d" % actr[0], list(shape), dt))
            for i in range(3):
                sbc("wfm%d" % i, [128, 16 * 128], BF16)
            for i in range(2):
                sbc("wtm%d" % i, [128, 16 * 512], BF16)
            sbc("xbuf", [128, 4, D])
            sbc("xb16", [128, 2, D], BF16)
            sbc("xT", [128, 16, 512], BF16)
            sbc("gtab", [128, D])
            sbc("btab", [128, D])
            sbc("stats", [128, 4, 4, 6])
            sbc("mv", [128, 4, 2])
            sbc("rstd", [128, 4])
            sbc("nmr", [128, 4])

        def load_fm(name, t):
            off, kc, cw, nt, per = lay[name]
            i = wctr["fm"] % 3
            wctr["fm"] += 1
            wt = B["wfm%d" % i]
            S.dma("sp", wt[:, 0:kc * 128], wview(name, t), reads=wkeys[name], writes=[("wfm", i)])
            return wt[:, 0:kc * 128].rearrange("p (k c) -> p k c", c=128), ("wfm", i)

        def load_tm(name, t):
            off, kc, cw, nt, per = lay[name]
            i = wctr["tm"] % 2
            wctr["tm"] += 1
            wt = B["wtm%d" % i]
            S.dma("sp", wt[:, 0:kc * 512], wview(name, t), reads=wkeys[name], writes=[("wtm", i)])
            return wt[:, 0:kc * 512].rearrange("p (k c) -> p k c", c=512), ("wtm", i)

        B["par"] = 0

        def kx(s_):
            return ("xbuf", B["par"], s_)

        def kxT(s_, kq_):
            return ("xT", B["par"], s_, kq_)

        def kTall():
            return [kxT(s_, kq_) for s_ in range(4) for kq_ in range(2)]

        def ln_gen(which, tr_dst=True, out_tile=None, out_writes=None, load_tabs=True):
            xbuf, xb16, xT, gtab, btab = B["xbuf"], B["xb16"], B["xT"], B["gtab"], B["btab"]
            stats, mv, rstd, nmr = B["stats"], B["mv"], B["rstd"], B["nmr"]
            par = B["par"]
            xks = [("xbuf", par, s_) for s_ in range(4)]
            if load_tabs:
                S.dma("sp", gtab[:], lng_d[which], writes=["gtab"])
                S.dma("sp", btab[:], lnb_d[which], writes=["btab"])
            for s in range(4):
                for c in range(4):
                    S.op("dve", lambda e, s=s, c=c: e.bn_stats(out=stats[:, s, c, :], in_=xbuf[:, s, c * 512:(c + 1) * 512]),
                         reads=[xks[s]], writes=[("stats", s, c)])
                yield
            for s in range(4):
                S.op("dve", lambda e, s=s: e.bn_aggr(out=mv[:, s, :], in_=stats[:, s, :, :].rearrange("p a b -> p (a b)")),
                     reads=[("stats", s, c) for c in range(4)], writes=[("mv", s)])
            mvk = [("mv", s) for s in range(4)]
            S.op("dve", lambda e: e.tensor_scalar_add(out=rstd[:], in0=mv[:, :, 1], scalar1=LN_EPS), reads=mvk, writes=["rstd"])
            S.op("act", lambda e: e.sqrt(out=rstd[:], in_=rstd[:]), reads=["rstd"], writes=["rstd"])
            S.op("dve", lambda e: e.reciprocal(out=rstd[:], in_=rstd[:]), reads=["rstd"], writes=["rstd"])
            S.op("dve", lambda e: e.scalar_tensor_tensor(out=nmr[:], in0=mv[:, :, 0], scalar=-1.0, in1=rstd[:],
                                                         op0=ALU.mult, op1=ALU.mult), reads=mvk + ["rstd"], writes=["nmr"])
            yield
            for s in range(4):
                S.op("act", lambda e, s=s: e.activation(out=xbuf[:, s, :], in_=xbuf[:, s, :], func=AF.Identity,
                                                        bias=nmr[:, s:s + 1], scale=rstd[:, s:s + 1]),
                     reads=[xks[s], "rstd", "nmr"], writes=[xks[s]])
            yield
            for s in range(4):
                S.op("dve", lambda e, s=s: e.tensor_tensor(out=xbuf[:, s, :], in0=xbuf[:, s, :], in1=gtab[:], op=ALU.mult),
                     reads=[xks[s], "gtab"], writes=[xks[s]])
                yield
            ykeys = []
            for s in range(4):
                ydst = xbuf[:, s, :] if out_tile is None else out_tile[:, s, :]
                ykey = xks[s] if out_tile is None else ("lnout", s)
                ykeys.append(ykey)
                S.op("pool", lambda e, s=s, ydst=ydst: e.tensor_tensor(out=ydst, in0=xbuf[:, s, :], in1=btab[:], op=ALU.add),
                     reads=[xks[s], "btab"], writes=[ykey] + (out_writes[s] if out_writes else []))
            yield
            if tr_dst:
                for s in range(4):
                    sl = s % 2
                    S.op("act", lambda e, s=s, sl=sl: e.copy(out=xb16[:, sl, :], in_=xbuf[:, s, :]), reads=[ykeys[s]], writes=[("xb16", sl)])
                    for kq in range(2):
                        pt, pk = nb()
                        S.tr([(pt[:, j * 128:(j + 1) * 128], xb16[:, sl, (kq * 8 + j) * 128:(kq * 8 + j + 1) * 128])
                              for j in range(8)], identb[:], reads=[("xb16", sl), "identb"], writes=[pk])
                        src = pt[:, :].rearrange("p (k c) -> p k c", c=128)
                        dst = xT[:, kq * 8:(kq + 1) * 8, s * 128:(s + 1) * 128]
                        if kq == 0:
                            S.op("dve", lambda e, src=src, dst=dst: e.tensor_copy(out=dst, in_=src),
                                 reads=[pk], writes=[("xT", par, s, kq)])
                        else:
                            S.op("act", lambda e, src=src, dst=dst: e.copy(out=dst, in_=src),
                                 reads=[pk], writes=[("xT", par, s, kq)])
                    yield

        def layer_norm_tile(which, **kw):
            for _ in ln_gen(which, **kw):
                pass

        def load_x_tile(i):
            for s in range(4):
                S.dma("sp", B["xbuf"][:, s, :], x_in[i * 512 + s * 128: i * 512 + (s + 1) * 128, :],
                      writes=[kx(s)])

        with ExitStack() as p1:
            def sb1(name, shape, dt=F32):
                return p1.enter_context(nc.sbuf_tensor(name, list(shape), dt))
            alloc_common(p1)
            xbufs = [B["xbuf"], sb1("xbuf_b", [128, 4, D])]
            xTs = [B["xT"], sb1("xT_b", [128, 16, 512], BF16)]

            def set_par(par):
                B["par"] = par
                B["xbuf"] = xbufs[par]
                B["xT"] = xTs[par]
            cosb = [sb1("cosb0", [32, 512])] * 2
            sinb = [sb1("sinb0", [32, 512])] * 2
            tb = [sb1("tb%d" % i, [128, 512], BF16) for i in range(2)]
            ra = sb1("ra", [32, 512])
            rb = sb1("rb", [32, 512])
            qst = [sb1("qst%d" % i, [128, 512], BF16) for i in range(3)]
            vst = [sb1("vst%d" % i_, [128, 4, 512], BF16) for i_ in range(2)]
            ust = sb1("ust", [64, 64, 8, 16], BF16)
            S.dma("sp", B["gtab"][:], lng_d[0], writes=["gtab"])
            S.dma("sp", B["btab"][:], lnb_d[0], writes=["btab"])
            set_par(0)
            load_x_tile(0)
            layer_norm_tile(0, load_tabs=False)
            for i in range(NTT):
                cur = i % 2
                set_par(cur)
                xT = B["xT"]
                cb, sn = cosb[cur], sinb[cur]
                S.dma("sp", cb[:], ropec_d[:, i * 512:(i + 1) * 512], writes=[("cosb", 0)])
                S.dma("sp", sn[:], ropes_d[:, i * 512:(i + 1) * 512], writes=[("sinb", 0)])

                def rot_finish(m, ps, pk):
                    d = DIL[(m % 12) // 4]
                    ld = 512 // d
                    q = qst[m % 3]
                    qk_ = ("qst", m % 3)
                    t_ = tb[m % 2]
                    ps2, pk2 = nf()
                    S.mm(ps2[:, :], [(pm[:, :], t_[:])], reads=["pm", ("tb", m % 2)], writes=[pk2])
                    S.op("dve", lambda e: e.tensor_tensor(out=ra[:], in0=ps[0:32, :], in1=cb[:], op=ALU.mult),
                         reads=[pk, ("cosb", 0)], writes=["ra"])
                    S.op("dve", lambda e: e.tensor_tensor(out=rb[:], in0=ps2[0:32, :], in1=sn[:], op=ALU.mult),
                         reads=[pk2, ("sinb", 0)], writes=["rb"])

                    def vw_out(ap):
                        return ap if d == 1 else ap.rearrange("p (r l) -> p l r", r=d)

                    def vw_in(ap):
                        return ap if d == 1 else ap.rearrange("p (l r) -> p l r", r=d)
                    S.op("act", lambda e: e.copy(out=vw_out(q[:, :]), in_=vw_in(ps[:, :])), reads=[pk], writes=[qk_])
                    S.op("dve", lambda e: e.tensor_tensor(out=vw_out(q[0:32, :]), in0=vw_in(ra[:]), in1=vw_in(rb[:]), op=ALU.add),
                         reads=["ra", "rb"], writes=[qk_])
                    qdst = qk_scr[m].rearrange("p (r l) -> p r l", r=d)[:, :, i * ld:(i + 1) * ld]
                    qsrc = q[:, :].rearrange("p (r l) -> p r l", r=d)
                    RC = 8 if d == 16 else d
                    for r0 in range(0, d, RC):
                        S.dma("act", qdst[:, r0:r0 + RC, :], qsrc[:, r0:r0 + RC, :], reads=[qk_], writes=[("qk_scr", m, r0)])

                pend = None
                for m in range(24):
                    w, wk = load_fm("qk", m)
                    ps, pk = nf()
                    S.mm(ps[:, :], [(w[:, k, :], xT[:, k, :]) for k in range(16)], reads=[wk] + kTall(), writes=[pk])
                    S.op("act", lambda e, ps=ps, m=m: e.copy(out=tb[m % 2][:], in_=ps[:, :]), reads=[pk], writes=[("tb", m % 2)])
                    if pend is not None:
                        rot_finish(*pend)
                    pend = (m, ps, pk)
                    if i + 1 < NTT and m >= 1:
                        set_par(1 - cur)
                        if m == 1:
                            load_x_tile(i + 1)
                            lng_ = ln_gen(0, load_tabs=False)
                        next(lng_, None)
                        set_par(cur)
                rot_finish(*pend)
                if i + 1 < NTT:
                    set_par(1 - cur)
                    for _ in lng_:
                        pass
                    set_par(cur)
                for c in range(3):
                    w, wk = load_tm("v", c)
                    vs_ = vst[c % 2]
                    for s in range(4):
                        ps, pk = nf()
                        S.mm(ps[:, :], [(xT[:, k, s * 128:(s + 1) * 128], w[:, k, :]) for k in range(16)],
                             reads=[wk, kxT(s, 0), kxT(s, 1)], writes=[pk])
                        if s % 2 == 0:
                            S.op("act", lambda e, ps=ps, s=s: e.copy(out=vs_[:, s, :], in_=ps[:, :]),
                                 reads=[pk], writes=[("vst", c % 2, s)])
                        else:
                            S.op("dve", lambda e, ps=ps, s=s: e.tensor_copy(out=vs_[:, s, :], in_=ps[:, :]),
                                 reads=[pk], writes=[("vst", c % 2, s)])
                    S.dma("act", v_scr[i * 512:(i + 1) * 512, c * 512:(c + 1) * 512].rearrange("(s p) c -> p s c", p=128), vs_[:],
                          reads=[("vst", c % 2, s) for s in range(4)], writes=[("v_scr", c)])
                for c in range(2):
                    w, wk = load_tm("u", c)
                    for sg in range(8):
                        ps, pk = nf()
                        S.mm(ps[0:64, :], [(xT[:, k, :].rearrange("p (n s) -> p s n", s=8)[:, sg, :], w[:, k, :])
                                           for k in range(16)], reads=[wk] + kTall(), writes=[pk])
                        dstv = ust[:, c * 32:(c + 1) * 32, sg, :]
                        srcv = ps[0:64, :].rearrange("p (g c) -> p g c", c=16)
                        if sg % 2 == 0:
                            S.op("act", lambda e, dstv=dstv, srcv=srcv: e.copy(out=dstv, in_=srcv), reads=[pk], writes=[("ust", sg, c)])
                        else:
                            S.op("dve", lambda e, dstv=dstv, srcv=srcv: e.tensor_copy(out=dstv, in_=srcv), reads=[pk], writes=[("ust", sg, c)])
                S.dma("act", u_scr[i * 64:(i + 1) * 64], ust[:],
                      reads=[("ust", sg, c) for sg in range(8) for c in range(2)], writes=["u_scr"])
            set_par(0)

        S.barrier()
        if stop_after in ("pass1", "p1ln", "p1qk", "p1v"):
            S.finish()
            return nc

        with ExitStack() as pss:
            def sbs(name, shape, dt=F32):
                return pss.enter_context(nc.sbuf_tensor(name, list(shape), dt))
            WS8 = sbs("WS8", [128, 256, 64], BF16)
            WC8 = sbs("WC8", [128, 128, 128], BF16)
            D8 = sbs("D8", [128, 64, 128], BF16)
            A1x = sbs("A1x", [128, 2, 64])
            A2x = sbs("A2x", [128, 2, 64])
            PI_ = math.pi

            def tt(en, out, a, b, op, reads, writes):
                S.op(en, lambda e: e.tensor_tensor(out=out, in0=a, in1=b, op=op), reads=reads, writes=writes)

            with ExitStack() as pp:
                def sbp(name, shape, dt=F32):
                    return pp.enter_context(nc.sbuf_tensor(name, list(shape), dt))
                names = ["LRE", "LIM", "DT", "M", "TH", "MAG", "MAGI", "TS", "TC", "SN", "CS", "ARE", "AIM", "IRE", "IIM",
                         "ERE", "DEN", "T", "NRE", "NIM", "CR", "CI", "K1"]
                V = {n: sbp("v_" + n, [128, 128]) for n in names}
                KI = sbp("KI", [128, 128], mybir.dt.int32)
                maskf = sbp("maskf_s", [128, 128])
                maskb = sbp("maskb_s", [128, 128])
                S.dma("sp", maskf[:], maskf_d, writes=["maskf"])
                S.dma("sp", maskb[:], maskb_d, writes=["maskb"])
                S.dma("sp", V["LRE"][:], lamre_d, writes=["LRE"])
                S.dma("sp", V["LIM"][:], lamim_d, writes=["LIM"])
                S.dma("sp", V["DT"][:], logdt_d, writes=["DT"])

                def v(n):
                    return V[n][:]

                def ew(out, a, b, op, en="dve"):
                    tt(en, v(out), v(a), v(b), op, [a, b], [out])

                def ts(out, a, s1, s2, op0, op1=None):
                    if op1 is None:
                        S.op("dve", lambda e: e.tensor_single_scalar(out=v(out), in_=v(a), scalar=s1, op=op0), reads=[a], writes=[out])
                    else:
                        S.op("dve", lambda e: e.tensor_scalar(out=v(out), in0=v(a), scalar1=s1, scalar2=s2, op0=op0, op1=op1),
                             reads=[a], writes=[out])

                def actf(out, a, func, scale=1.0):
                    S.op("act", lambda e: e.activation(out=v(out), in_=v(a), func=func, scale=scale), reads=[a], writes=[out])

                actf("DT", "DT", AF.Exp)
                ts("LRE", "LRE", -1e-4, None, ALU.min)
                ew("M", "LRE", "DT", ALU.mult)
                ew("TH", "LIM", "DT", ALU.mult)
                actf("MAG", "M", AF.Exp)
                actf("MAGI", "M", AF.Exp, scale=-1.0)

                def reduce_angle(out, shift):
                    ts("K1", "TH", shift, 1.0 / TWO_PI, ALU.add, ALU.mult)
                    S.op("dve", lambda e: e.tensor_copy(out=KI[:], in_=v("K1")), reads=["K1"], writes=["KI"])
                    S.op("dve", lambda e: e.tensor_copy(out=v("K1"), in_=KI[:]), reads=["KI"], writes=["K1"])
                    ts("K1", "K1", -TWO_PI, None, ALU.mult)
                    ts(out, "TH", shift, None, ALU.add)
                    ew(out, out, "K1", ALU.add)
                    ts("K1", out, PI_, TWO_PI, ALU.is_gt, ALU.mult)
                    ew(out, out, "K1", ALU.subtract)
                    ts("K1", out, -PI_, TWO_PI, ALU.is_lt, ALU.mult)
                    ew(out, out, "K1", ALU.add)
                    ts(out, out, PI_, None, ALU.min)
                    ts(out, out, -PI_, None, ALU.max)

                reduce_angle("TS", 0.0)
                reduce_angle("TC", PI_ / 2.0)
                actf("SN", "TS", AF.Sin)
                actf("CS", "TC", AF.Sin)
                ew("ARE", "MAG", "CS", ALU.mult)
                ew("AIM", "MAG", "SN", ALU.mult)
                ew("IRE", "MAGI", "CS", ALU.mult)
                ew("IIM", "MAGI", "SN", ALU.mult)
                ts("IIM", "IIM", -1.0, None, ALU.mult)
                ts("ERE", "ARE", -1.0, None, ALU.add)
                ew("DEN", "LRE", "LRE", ALU.mult)
                ew("T", "LIM", "LIM", ALU.mult)
                ew("DEN", "DEN", "T", ALU.add)
                S.op("dve", lambda e: e.reciprocal(out=v("DEN"), in_=v("DEN")), reads=["DEN"], writes=["DEN"])
                ew("NRE", "ERE", "LRE", ALU.mult)
                ew("T", "AIM", "LIM", ALU.mult)
                ew("NRE", "NRE", "T", ALU.add)
                ew("NIM", "AIM", "LRE", ALU.mult)
                ew("T", "ERE", "LIM", ALU.mult)
                ew("NIM", "NIM", "T", ALU.subtract)
                ew("CR", "NRE", "DEN", ALU.mult)
                ew("CI", "NIM", "DEN", ALU.mult)
                PAr = sbp("PAr", [128, 9, 128])
                PAi = sbp("PAi", [128, 9, 128])
                PIr = sbp("PIr", [128, 9, 128])
                PIi = sbp("PIi", [128, 9, 128])
                for (Pr, Pi, nr, ni, tag) in ((PAr, PAi, "ARE", "AIM", "PA"), (PIr, PIi, "IRE", "IIM", "PI")):
                    S.op("dve", lambda e, Pr=Pr: e.memset(Pr[:, 0, :], 1.0), writes=[(tag, "r", 0)])
                    S.op("dve", lambda e, Pi=Pi: e.memset(Pi[:, 0, :], 0.0), writes=[(tag, "i", 0)])
                    for k in range(1, 9):
                        pr, pi_ = Pr[:, k - 1, :], Pi[:, k - 1, :]
                        kr, ki = (tag, "r", k - 1), (tag, "i", k - 1)
                        tt("dve", v("T"), pr, v(nr), ALU.mult, [kr, nr], ["T"])
                        tt("dve", v("K1"), pi_, v(ni), ALU.mult, [ki, ni], ["K1"])
                        tt("dve", Pr[:, k, :], v("T"), v("K1"), ALU.subtract, ["T", "K1"], [(tag, "r", k)])
                        tt("dve", v("T"), pr, v(ni), ALU.mult, [kr, ni], ["T"])
                        tt("dve", v("K1"), pi_, v(nr), ALU.mult, [ki, nr], ["K1"])
                        tt("dve", Pi[:, k, :], v("T"), v("K1"), ALU.add, ["T", "K1"], [(tag, "i", k)])
                PAk = [("PA", c, k) for c in "ri" for k in range(9)]
                PIk = [("PI", c, k) for c in "ri" for k in range(9)]
                for dr in range(2):
                    for (dst, src, sgn_lo, key) in ((A1x, PAr, 1.0, "A1x"), (A2x, PAi, -1.0, "A2x")):
                        for hf in range(2):
                            rows = slice(hf * 64, (hf + 1) * 64)
                            srcv = src[rows, 8, dr * 64:(dr + 1) * 64].rearrange("p (a two) -> p a two", two=2)[:, :, hf]
                            S.op("dve", lambda e, dst=dst, rows=rows, srcv=srcv, dr=dr, sgn_lo=sgn_lo: e.tensor_single_scalar(
                                out=dst[rows, dr, 0:32], in_=srcv, scalar=sgn_lo, op=ALU.mult), reads=PAk, writes=[(key, dr, hf, 0)])
                            S.op("dve", lambda e, dst=dst, rows=rows, srcv=srcv, dr=dr: e.tensor_copy(
                                out=dst[rows, dr, 32:64], in_=srcv), reads=PAk, writes=[(key, dr, hf, 1)])
                Br = sbp("Br", [128, 128, 16])
                Bi = sbp("Bi", [128, 128, 16])
                Cr = sbp("Cr", [128, 128, 16])
                Ci = sbp("Ci", [128, 128, 16])
                T1 = sbp("T1", [128, 16, 8, 16])
                T2 = sbp("T2", [128, 16, 8, 16])
                X1v = T1[:].rearrange("p a b c -> p (a b) c")
                X2v = T2[:].rearrange("p a b c -> p (a b) c")
                S.dma("sp", Br[:], bre_d.rearrange("p d g c -> p (d g) c"), writes=["Br"])
                S.dma("sp", Bi[:], bim_d.rearrange("p d g c -> p (d g) c"), writes=["Bi"])
                S.dma("sp", Cr[:], cre_d.rearrange("p d g c -> p (d g) c"), writes=["Cr"])
                S.dma("sp", Ci[:], cim_d.rearrange("p d g c -> p (d g) c"), writes=["Ci"])
                crb = v("CR").unsqueeze(2).to_broadcast([128, 128, 16])
                cib = v("CI").unsqueeze(2).to_broadcast([128, 128, 16])
                tt("dve", X1v, Br[:], crb, ALU.mult, ["Br", "CR"], ["T1"])
                tt("pool", X2v, Bi[:], cib, ALU.mult, ["Bi", "CI"], ["T2"])
                tt("dve", X1v, X1v, X2v, ALU.subtract, ["T1", "T2"], ["T1"])
                tt("pool", X2v, Br[:], cib, ALU.mult, ["Br", "CI"], ["T2"])
                tt("dve", Bi[:], Bi[:], crb, ALU.mult, ["Bi", "CR"], ["Bi"])
                tt("dve", Bi[:], Bi[:], X2v, ALU.add, ["Bi", "T2"], ["Bi"])
                S.op("dve", lambda e: e.tensor_copy(out=Br[:], in_=X1v), reads=["T1"], writes=["Br"])
                GQ = 16
                XSr = sbp("XSr", [128, GQ, 8, 16])
                XSi = sbp("XSi", [128, GQ, 8, 16])
                YDr = sbp("YDr", [128, GQ, 8, 16])
                YDi = sbp("YDi", [128, GQ, 8, 16])
                SELr = sbp("SELr", [128, GQ, 8])
                SELi = sbp("SELi", [128, GQ, 8])
                D8a = sbp("D8a", [128, GQ, 128])
                full = [128, GQ, 8, 16]

                def build_sel(Pr, Pi, pk, exps, cols):
                    for sgm in range(8):
                        S.op("dve", lambda e, sgm=sgm: e.tensor_copy(out=SELr[:, :, sgm], in_=Pr[:, exps[sgm], cols]), reads=pk, writes=[("SELr", sgm)])
                        S.op("pool", lambda e, sgm=sgm: e.tensor_copy(out=SELi[:, :, sgm], in_=Pi[:, exps[sgm], cols]), reads=pk, writes=[("SELi", sgm)])

                selk = [("SELr", q_) for q_ in range(8)] + [("SELi", q_) for q_ in range(8)]

                def cmul_tab(outr, outi, okr, oki, Mr, Mi, mkeys, cols):
                    sr = SELr[:, :, :].unsqueeze(3).to_broadcast(full)
                    si = SELi[:, :, :].unsqueeze(3).to_broadcast(full)
                    mr = Mr[:, cols, :].unsqueeze(2).to_broadcast(full)
                    mi = Mi[:, cols, :].unsqueeze(2).to_broadcast(full)
                    tt("dve", T1[:], sr, mr, ALU.mult, selk + mkeys, ["T1"])
                    tt("pool", T2[:], si, mi, ALU.mult, selk + mkeys, ["T2"])
                    tt("dve", outr[:], T1[:], T2[:], ALU.subtract, ["T1", "T2"], [okr])
                    tt("dve", T1[:], sr, mi, ALU.mult, selk + mkeys, ["T1"])
                    tt("pool", T2[:], si, mr, ALU.mult, selk + mkeys, ["T2"])
                    tt("dve", outi[:], T1[:], T2[:], ALU.add, ["T1", "T2"], [oki])

                for gq in range(64 // GQ):
                    for dr in range(2):
                        g0 = gq * GQ
                        cols = slice(dr * 64 + g0, dr * 64 + g0 + GQ)
                        build_sel(PAr, PAi, PAk, [7 - q_ for q_ in range(8)] if dr == 0 else list(range(8)), cols)
                        cmul_tab(XSr, XSi, "XSr", "XSi", Br, Bi, ["Br", "Bi"], cols)
                        for reim, (XS, xk) in enumerate(((XSr, "XSr"), (XSi, "XSi"))):
                            for q8 in range(GQ // 8):
                                ps, pk = nf()
                                S.tr([(ps[:, j * 64:(j + 1) * 64], XS[0:64, q8 * 8 + j, :, :].rearrange("p a b -> p (a b)")) for j in range(8)],
                                     identf[0:64, 0:64], reads=[xk, "identf"], writes=[pk])
                                i0 = (dr * 2 + reim) * 64 + g0 + q8 * 8
                                S.op("act", lambda e, ps=ps, i0=i0: e.copy(
                                    out=WS8[:, i0:i0 + 8, :], in_=ps[:, :].rearrange("p (j c) -> p j c", c=64)),
                                    reads=[pk], writes=[("WS8", i0)])
                        S.op("dve", lambda e: e.tensor_copy(out=XSr[64:128], in_=XSi[64:128]), reads=["XSi"], writes=["XSr"])
                        build_sel(PIr, PIi, PIk, [7 - q_ for q_ in range(8)] if dr == 0 else list(range(8)), cols)
                        cmul_tab(YDr, YDi, "YDr", "YDi", Cr, Ci, ["Cr", "Ci"], cols)
                        S.op("dve", lambda e: e.tensor_single_scalar(out=YDr[64:128], in_=YDi[64:128], scalar=-1.0, op=ALU.mult),
                             reads=["YDi"], writes=["YDr"])
                        for q4 in range(GQ // 4):
                            ps, pk = nf()
                            for j in range(4):
                                g = q4 * 4 + j
                                S.mm(ps[:, j * 128:(j + 1) * 128],
                                     [(XSr[:, g, :, :].rearrange("p a b -> p (a b)"), YDr[:, g, :, :].rearrange("p a b -> p (a b)"))],
                                     reads=["XSr", "YDr"], writes=[pk])
                            msk, mk = (maskf, "maskf") if dr == 0 else (maskb, "maskb")
                            mb_ = msk[:, :].unsqueeze(1).to_broadcast([128, 4, 128])
                            psv = ps[:, :].rearrange("p (j c) -> p j c", c=128)
                            dk = ("D8a", q4)
                            if dr == 0:
                                tt("dve", D8a[:, q4 * 4:(q4 + 1) * 4, :], psv, mb_, ALU.mult, [pk, mk], [dk])
                            else:
                                tt("dve", T1[:, 0:4, :, :].rearrange("p a b c -> p a (b c)"), psv, mb_, ALU.mult, [pk, mk], ["T1"])
                                tt("dve", D8[:, g0 + q4 * 4:g0 + (q4 + 1) * 4, :], D8a[:, q4 * 4:(q4 + 1) * 4, :],
                                   T1[:, 0:4, :, :].rearrange("p a b c -> p a (b c)"), ALU.add, [dk, "T1"], [("D8", g0 + q4 * 4)])
                        build_sel(PAr, PAi, PAk, [q_ + 1 for q_ in range(8)] if dr == 0 else [8 - q_ for q_ in range(8)], cols)
                        cmul_tab(YDr, YDi, "YDr", "YDi", Cr, Ci, ["Cr", "Ci"], cols)
                        for reim, (WT, wk_, sgn) in enumerate(((YDr, "YDr", 1.0), (YDi, "YDi", -1.0))):
                            i0 = (dr * 2 + reim) * 32 + g0 // 2
                            for hf in range(2):
                                rows = slice(hf * 64, (hf + 1) * 64)
                                srcv = WT[rows].rearrange("p (a two) t c -> p a two (t c)", two=2)[:, :, hf, :]
                                S.op("dve", lambda e, rows=rows, srcv=srcv, i0=i0, sgn=sgn: e.tensor_single_scalar(
                                    out=WC8[rows, i0:i0 + GQ // 2, :], in_=srcv, scalar=sgn, op=ALU.mult),
                                    reads=[wk_], writes=[("WC8", i0, hf)])
            S.barrier()
            if stop_after == "ssmprep":
                if debug:
                    dbg_ws = nc.dram_tensor("dbg_ws", [128, 256, 64], BF16, kind="ExternalOutput").ap()
                    dbg_wc = nc.dram_tensor("dbg_wc", [128, 128, 128], BF16, kind="ExternalOutput").ap()
                    dbg_d8 = nc.dram_tensor("dbg_d8", [128, 64, 128], BF16, kind="ExternalOutput").ap()
                    dbg_a = nc.dram_tensor("dbg_a", [128, 4, 64], F32, kind="ExternalOutput").ap()
                    S.dma("sp", dbg_ws, WS8[:], writes=["dbg1"])
                    S.dma("sp", dbg_wc, WC8[:], writes=["dbg2"])
                    S.dma("sp", dbg_d8, D8[:], writes=["dbg3"])
                    S.dma("sp", dbg_a[:, 0:2, :], A1x[:], writes=["dbg4"])
                    S.dma("sp", dbg_a[:, 2:4, :], A2x[:], writes=["dbg5"])
                S.finish()
                return nc
            Ut = sbs("Ut", [64, 64, 8, 16], BF16)
            UG = sbs("UG", [128, 64, 64], BF16)
            Sb = sbs("Sb", [128, 64, 64])
            HZ = sbs("HZ", [128, 65, 64])
            HF16 = sbs("HF16", [128, 64, 64], BF16)
            HB16 = sbs("HB16", [128, 64, 64], BF16)
            m1 = sbs("m1", [128, 64])
            m2 = sbs("m2", [128, 64])
            lkt = sbs("lkt", [128, NBLK + 1])
            DSK = sbs("DSK", [128, 64])
            BGL = sbs("BGL", [128, 8])
            yr = sbs("yr", [128, 8, 64])
            ysq = sbs("ysq", [128, 8, 64])
            ysg = sbs("ysg", [128, 8, 64])
            yact = [sbs("yact%d" % i, [128, 8, 64], BF16) for i in range(2)]
            Yt = sbs("Yt", [64, 8, 1024], BF16)
            yT = sbs("yT", [128, 8, 512], BF16)
            ssmst = sbs("ssmst", [128, 8, 512], BF16)
            wgl = [sbs("wgl%d" % i, [128, 8, 128], BF16) for i in range(2)]
            S.dma("sp", lkt[:], lk_d, writes=["lkt"])
            S.dma("sp", DSK[:], dsk_d, writes=["DSK"])
            S.dma("sp", BGL[:], bglu_d, writes=["BGL"])
            WS8k = [("WS8", i0) for i0 in range(0, 256, 8)]
            WC8k = [("WC8", i0, hf) for i0 in range(0, 128, 8) for hf in range(2)]
            D8k = [("D8", g) for g in range(0, 64, 4)]
            UGk = [("UG", q) for q in range(8)]

            def load_U(bk):
                S.dma("sp", Ut[:], u_scr[bk * 64:(bk + 1) * 64], reads=["u_scr"], writes=["Ut"])
                for g8 in range(8):
                    pt, pk = nb()
                    S.tr([(pt[:, j * 64:(j + 1) * 64], Ut[0:64, g8 * 8 + j, :, :].rearrange("p a b -> p (a b)")) for j in range(8)],
                         identb[0:64, 0:64], reads=["Ut", "identb"], writes=[pk])
                    src = pt[:, 0:512].rearrange("p (j n) -> p j n", n=64)
                    if g8 % 2 == 0:
                        S.op("act", lambda e, src=src, g8=g8: e.copy(out=UG[:, g8 * 8:(g8 + 1) * 8, :], in_=src), reads=[pk], writes=[("UG", g8)])
                    else:
                        S.op("dve", lambda e, src=src, g8=g8: e.tensor_copy(out=UG[:, g8 * 8:(g8 + 1) * 8, :], in_=src), reads=[pk], writes=[("UG", g8)])

            def compute_S(dr):
                for q8 in range(8):
                    ps, pk = nf()
                    for reim in range(2):
                        for j4 in range(4):
                            for g2 in range(2):
                                g = (q8 * 4 + j4) * 2 + g2
                                sl = reim * 4 + j4
                                S.mm(ps[g2 * 64:(g2 + 1) * 64, sl * 64:(sl + 1) * 64],
                                     [(WS8[:, (dr * 2 + reim) * 64 + g, :], UG[:, g, :])], reads=WS8k + UGk, writes=[pk])
                    for reim in range(2):
                        src = ps[:, reim * 256:(reim + 1) * 256].rearrange("p (j n) -> p n j", n=64)
                        dst = Sb[:, :, reim * 32 + q8 * 4: reim * 32 + q8 * 4 + 4]
                        if reim == 0:
                            S.op("act", lambda e, src=src, dst=dst: e.copy(out=dst, in_=src), reads=[pk], writes=[("Sb", q8, reim)])
                        else:
                            S.op("dve", lambda e, src=src, dst=dst: e.tensor_copy(out=dst, in_=src), reads=[pk], writes=[("Sb", q8, reim)])

            Sbk = [("Sb", q8, reim) for q8 in range(8) for reim in range(2)]

            def scan(dr):
                order = range(64) if dr == 0 else range(63, -1, -1)
                for jn in order:
                    pv, nw = (jn, jn + 1) if dr == 0 else (jn + 1, jn)
                    tt("dve", m1[:], A1x[:, dr, :], HZ[:, pv, 0:64], ALU.mult, [("HZ", pv)], ["m1"])
                    tt("dve", m2[:, 0:32], A2x[:, dr, 0:32], HZ[:, pv, 32:64], ALU.mult, [("HZ", pv)], ["m2a"])
                    tt("dve", m2[:, 32:64], A2x[:, dr, 32:64], HZ[:, pv, 0:32], ALU.mult, [("HZ", pv)], ["m2b"])
                    tt("dve", m1[:], m1[:], m2[:], ALU.add, ["m1", "m2a", "m2b"], ["m1"])
                    tt("dve", HZ[:, nw, 0:64], m1[:], Sb[:, jn, :], ALU.add, ["m1"] + Sbk, [("HZ", nw)])

            HZk = [("HZ", q) for q in range(65)]
            S.op("dve", lambda e: e.memset(HB16[:], 0.0), writes=["HB16"])
            S.dma("act", hb_scr[:, NSUP:NSUP + 1, :], HB16[:, 0:1, :], reads=["HB16"], writes=["hb_scr"])
            S.op("dve", lambda e: e.memset(HZ[:, 64, :], 0.0), writes=[("HZ", 64)])
            for bk in range(NBLK - 1, -1, -1):
                load_U(bk)
                compute_S(1)
                scan(1)
                S.op("dve", lambda e, bk=bk: e.tensor_scalar(out=HZ[:, 0, :], in0=HZ[:, 0, :], scalar1=lkt[:, bk:bk + 1], scalar2=None,
                                                            op0=ALU.mult), reads=[("HZ", 0), "lkt"], writes=[("HZ", 0)])
                S.op("act", lambda e: e.copy(out=HB16[:], in_=HZ[:, 0:64, 0:64]), reads=HZk, writes=["HB16"])
                S.dma("act", hb_scr[:, bk * 64:(bk + 1) * 64, :], HB16[:], reads=["HB16"], writes=["hb_scr"])
                S.op("dve", lambda e: e.tensor_copy(out=HZ[:, 64, :], in_=HZ[:, 0, :]), reads=[("HZ", 0)] + HZk, writes=[("HZ", 64)])
            S.op("dve", lambda e: e.memset(HZ[:, 0, :], 0.0), reads=HZk, writes=[("HZ", 0)])
            for bk in range(NBLK):
                load_U(bk)
                compute_S(0)
                scan(0)
                S.op("act", lambda e: e.copy(out=HF16[:], in_=HZ[:, 0:64, 0:64]), reads=HZk, writes=["HF16"])
                S.dma("sp", HB16[:], hb_scr[:, bk * 64 + 1: bk * 64 + 65, :], reads=["hb_scr"], writes=["HB16"])
                S.op("dve", lambda e, bk=bk: e.tensor_scalar(out=HZ[:, 0, :], in0=HZ[:, 64, :], scalar1=lkt[:, bk + 1:bk + 2], scalar2=None,
                                                            op0=ALU.mult), reads=HZk + ["lkt"], writes=[("HZ", 0)])
                def y_front(g8):
                    py, pyk = nf()
                    for j in range(8):
                        g = g8 * 8 + j
                        pair = g // 2
                        rows = slice(64 * (g % 2), 64 * (g % 2) + 64)
                        prs = [(D8[:, g, :], UG[:, g, :])]
                        for reim in range(2):
                            prs.append((WC8[rows, (0 * 2 + reim) * 32 + pair, :], HF16[rows, :, reim * 32 + pair]))
                        for reim in range(2):
                            prs.append((WC8[rows, (1 * 2 + reim) * 32 + pair, :], HB16[rows, :, reim * 32 + pair]))
                        S.mm(py[:, j * 64:(j + 1) * 64], prs, reads=D8k + WC8k + UGk + ["HF16", "HB16"], writes=[pyk])
                    ya = yact[g8 % 2]
                    yak = ("yact", g8 % 2)
                    pyv = py[:, :].rearrange("p (j n) -> p j n", n=64)
                    tt("dve", yr[:], UG[:, g8 * 8:(g8 + 1) * 8, :], DSK[:, g8 * 8:(g8 + 1) * 8].unsqueeze(2).to_broadcast([128, 8, 64]),
                       ALU.mult, UGk + ["DSK"], ["yr"])
                    tt("dve", yr[:], yr[:], pyv, ALU.add, ["yr", pyk], ["yr"])
                    S.op("act", lambda e: e.activation(out=ysq[:], in_=yr[:], func=AF.Square), reads=["yr"], writes=["ysq"])
                    S.op("dve", lambda e: e.tensor_scalar(out=ysq[:], in0=ysq[:], scalar1=0.044715, scalar2=1.0, op0=ALU.mult, op1=ALU.add),
                         reads=["ysq"], writes=["ysq"])
                    tt("dve", ysq[:], ysq[:], yr[:], ALU.mult, ["ysq", "yr"], ["ysq"])
                    S.op("act", lambda e: e.activation(out=ysg[:], in_=ysq[:], func=AF.Sigmoid, scale=1.5957691216057308),
                         reads=["ysq"], writes=["ysg"])
                    tt("dve", ya[:], yr[:], ysg[:], ALU.mult, ["yr", "ysg"], [yak])
                    return (g8, ya, yak)

                def y_back(g8, ya, yak):
                    pt, pk = nb()
                    S.tr([(pt[0:64, j * 128:(j + 1) * 128], ya[:, j, :]) for j in range(8)], identb[:], reads=[yak, "identb"], writes=[pk])
                    S.op("act", lambda e: e.copy(
                        out=Yt[:, :, g8 * 128:(g8 + 1) * 128].rearrange("n t (j c) -> n j t c", c=16),
                        in_=pt[0:64, :].rearrange("n (j t c) -> n j t c", j=8, t=8)), reads=[pk], writes=[("Yt", g8)])

                ypend = None
                for g8 in range(8):
                    ycur = y_front(g8)
                    if ypend is not None:
                        y_back(*ypend)
                    ypend = ycur
                y_back(*ypend)
                Ytk = [("Yt", q) for q in range(8)]
                for jt in range(8):
                    pt, pk = nb()
                    S.tr([(pt[:, t_ * 64:(t_ + 1) * 64], Yt[0:64, t_, jt * 128:(jt + 1) * 128]) for t_ in range(8)],
                         identb[0:64, 0:64], reads=Ytk + ["identb"], writes=[pk])
                    src = pt[:, 0:512].rearrange("p (t n) -> p t n", n=64)
                    dst = yT[:, jt, :].rearrange("p (n t) -> p t n", t=8)
                    if jt % 2 == 0:
                        S.op("act", lambda e, src=src, dst=dst: e.copy(out=dst, in_=src), reads=[pk], writes=[("yT", jt)])
                    else:
                        S.op("dve", lambda e, src=src, dst=dst: e.tensor_copy(out=dst, in_=src), reads=[pk], writes=[("yT", jt)])
                yTk = [("yT", q) for q in range(8)]
                for m in range(8):
                    wg = wgl[m % 2]
                    S.dma("sp", wg[:], wview("glu", m).rearrange("p (k c) -> p k c", c=128), reads=wkeys["glu"], writes=[("wgl", m % 2)])
                    ps, pk = nf()
                    S.mm(ps[:, :], [(wg[:, k, :], yT[:, k, :]) for k in range(8)], reads=[("wgl", m % 2)] + yTk, writes=[pk])
                    S.op("act", lambda e, ps=ps, m=m: e.activation(
                        out=ysq[:].rearrange("p a b -> p (a b)"), in_=ps[:, :], func=AF.Sigmoid, bias=BGL[:, m:m + 1]),
                         reads=[pk, "BGL"], writes=["ysq"])
                    tt("dve", ssmst[:, m, :], yT[:, m, :], ysq[:].rearrange("p a b -> p (a b)"), ALU.mult, [("yT", m), "ysq"], [("ssmst", m)])
                S.dma("act", ssm_scr[:, :, bk * 512:(bk + 1) * 512].rearrange("j p t -> p j t"), ssmst[:],
                      reads=[("ssmst", q) for q in range(8)], writes=["ssm_scr"])
        S.barrier()
        if stop_after == "ssm":
            S.finish()
            return nc

        with ExitStack() as pa:
            def sba(name, shape, dt=F32):
                return pa.enter_context(nc.sbuf_tensor(name, list(shape), dt))
            NB = NT // 128
            amask = sba("amask_s", [128, 3, 128], BF16)
            amaskl = sba("amaskl_s", [128, 3, 128], BF16)
            S.dma("sp", amask[:], amask_d, writes=["amask"])
            S.dma("sp", amaskl[:], amaskl_d, writes=["amaskl"])
            qT = [sba("aq%d" % i, [128, NT], BF16) for i in range(2)]
            kT = [sba("ak%d" % i, [128, NT], BF16) for i in range(2)]
            vA = [sba("av%d" % i, [128, NB, 129], BF16) for i in range(2)]
            ost = [sba("aost%d" % i, [128, NB, 129]) for i in range(2)]
            pexp = [sba("pexp%d" % i, [128, 3, 128], BF16) for i in range(3)]
            for i in range(2):
                S.op("dve", lambda e, i=i: e.memset(vA[i][:, :, 128:129], 1.0), writes=[("vAone", i)])
            for h in range(NH):
                d = DIL[h // 4]
                L = NT // d
                nbk = L // 128
                half = nbk // 2
                b = h % 2
                S.dma("sp", qT[b][:], qk_scr[h], reads=[("qk_scr", h, 0), ("qk_scr", h, 8)], writes=[("aq", b)])
                S.dma("sp", kT[b][:], qk_scr[12 + h], reads=[("qk_scr", 12 + h, 0), ("qk_scr", 12 + h, 8)], writes=[("ak", b)])
                vsrc = v_scr[:, h * 128:(h + 1) * 128].rearrange("(j m r) e -> r m j e", m=128, r=d)
                JC = 8
                for r in range(d):
                    for j0 in range(0, nbk, JC):
                        j1 = min(nbk, j0 + JC)
                        S.dma("sp", vA[b][:, r * nbk + j0:r * nbk + j1, 0:128], vsrc[r][:, j0:j1, :],
                              reads=[("v_scr", 0), ("v_scr", 1), ("v_scr", 2)], writes=[("vA", b, (r * nbk + j0) // JC)])
                def front(r, j):
                    blk = r * nbk + j
                    jjs = [jj for jj in (j - 1, j, j + 1) if 0 <= jj < nbk]
                    ps, pk = nf()
                    for jj in jjs:
                        dj = jj - j + 1
                        S.mm(ps[:, dj * 128:(dj + 1) * 128],
                             [(kT[b][:, (r * nbk + jj) * 128:(r * nbk + jj + 1) * 128], qT[b][:, blk * 128:(blk + 1) * 128])],
                             reads=[("ak", b), ("aq", b)], writes=[pk])
                    lo = jjs[0] - j + 1
                    hi = jjs[-1] - j + 2
                    pe_ = pexp[blk % 3]
                    pek = ("pexp", blk % 3)
                    S.op("act", lambda e: e.activation(
                        out=pe_[:, lo:hi, :], in_=ps[:, lo * 128:hi * 128].rearrange("p (a c) -> p a c", c=128),
                        func=AF.Exp, scale=SCALE), reads=[pk], writes=[pek])
                    crosses = [(j < half) != (jj < half) for jj in jjs]
                    if not any(crosses):
                        S.op("dve", lambda e: e.tensor_tensor(out=pe_[:, lo:hi, :], in0=pe_[:, lo:hi, :], in1=amask[:, lo:hi, :], op=ALU.mult),
                             reads=[pek, "amask"], writes=[pek])
                    else:
                        for jj, cross in zip(jjs, crosses):
                            dj = jj - j + 1
                            msk, mk = (amaskl, "amaskl") if cross else (amask, "amask")
                            S.op("dve", lambda e, dj=dj, msk=msk: e.tensor_tensor(
                                out=pe_[:, dj, :], in0=pe_[:, dj, :], in1=msk[:, dj, :], op=ALU.mult),
                                reads=[pek, mk], writes=[pek])
                    return (r, j, blk, jjs, pe_, pek)

                def back(r, j, blk, jjs, pe_, pek):
                    po, pok = nf()
                    S.mm(po[:, 0:129], [(pe_[:, jj - j + 1, :], vA[b][:, r * nbk + jj, :]) for jj in jjs],
                         reads=[pek, ("vAone", b)] + [("vA", b, (r * nbk + jj) // JC) for jj in jjs], writes=[pok])
                    ok_ = ("ost", b, blk % 4)
                    if blk % 2 == 0:
                        S.op("act", lambda e: e.copy(out=ost[b][:, blk, :], in_=po[:, 0:129]), reads=[pok], writes=[ok_])
                    else:
                        S.op("dve", lambda e: e.tensor_copy(out=ost[b][:, blk, :], in_=po[:, 0:129]), reads=[pok], writes=[ok_])

                pend_ = None
                for r in range(d):
                    for j in range(nbk):
                        cur_ = front(r, j)
                        if pend_ is not None:
                            back(*pend_)
                        pend_ = cur_
                back(*pend_)
                ndst = nd_scr[:, h, :].rearrange("(j m r) e -> r m j e", m=128, r=d)
                for r in range(d):
                    for j0 in range(0, nbk, JC):
                        j1 = min(nbk, j0 + JC)
                        S.dma("act", ndst[r][:, j0:j1, :], ost[b][:, r * nbk + j0:r * nbk + j1, :],
                              reads=[("ost", b, q_) for q_ in range(4)], writes=[("nd_scr", h, r, j0)])
        S.barrier()
        if stop_after == "attn":
            S.finish()
            return nc

        with ExitStack() as p2:
            def sb2(name, shape, dt=F32):
                return p2.enter_context(nc.sbuf_tensor(name, list(shape), dt))
            alloc_common(p2)
            xbuf, xT = B["xbuf"], B["xT"]
            big = sb2("big", [128, 64, 512], BF16)
            ndts = [sb2("ndt%d" % i_, [128, 12, 129]) for i_ in range(2)]
            rdens = [sb2("rden%d" % i_, [128, 4]) for i_ in range(2)]
            att16s = [sb2("att16_0", [128, 1536], BF16)] * 2
            sg = [sb2("sg%d" % i, [128, 512]) for i in range(2)]
            t1 = sb2("t1", [128, 512])
            pst = sb2("pst", [128, 256])
            p16 = sb2("p16", [128, 256], BF16)
            pT = sb2("pT", [128, 2, 512], BF16)
            sgc = [0]

            def nsg():
                i = sgc[0] % 2
                sgc[0] += 1
                return sg[i], ("sg", i)

            nd_keys = [k_ for k_ in S.R if isinstance(k_, tuple) and k_[0] == "nd_scr"]
            def nd_load(i, s):
                S.dma("sp", ndts[s % 2][:], nd_scr[i * 512 + s * 128: i * 512 + (s + 1) * 128], reads=nd_keys, writes=[("ndt", s % 2)])

            for i in range(NTT):
                load_x_tile(i)
                nd_load(i, 0)
                nd_load(i, 1)
                layer_norm_tile(0)
                for s in range(4):
                    ndt, rden, att16 = ndts[s % 2], rdens[s % 2], att16s[s % 2]
                    nk_, rk_, ak_ = ("ndt", s % 2), ("rden", s % 2), ("att16", 0)
                    S.op("dve", lambda e: e.tensor_tensor(out=rden[:], in0=ndt[:, 0:4, 128], in1=ndt[:, 4:8, 128], op=ALU.add),
                         reads=[nk_], writes=[rk_])
                    S.op("dve", lambda e: e.tensor_tensor(out=rden[:], in0=rden[:], in1=ndt[:, 8:12, 128], op=ALU.add),
                         reads=[nk_, rk_], writes=[rk_])
                    S.op("dve", lambda e: e.reciprocal(out=rden[:], in_=rden[:]), reads=[rk_], writes=[rk_])
                    for g3 in range(3):
                        S.op("dve", lambda e, g3=g3: e.tensor_tensor(
                            out=att16[:, g3 * 512:(g3 + 1) * 512].rearrange("p (j e) -> p j e", e=128),
                            in0=ndt[:, g3 * 4:(g3 + 1) * 4, 0:128],
                            in1=rden[:, :].unsqueeze(2).to_broadcast([128, 4, 128]), op=ALU.mult),
                            reads=[nk_, rk_], writes=[ak_])
                    if s + 2 < 4:
                        nd_load(i, s + 2)
                    for (k0, nk) in ((0, 8), (8, 4)):
                        pt, pk = nb()
                        S.tr([(pt[:, j * 128:(j + 1) * 128], att16[:, (k0 + j) * 128:(k0 + j + 1) * 128]) for j in range(nk)],
                             identb[:], reads=[ak_, "identb"], writes=[pk])
                        S.op("act", lambda e, pt=pt, k0=k0, nk=nk, s=s: e.copy(
                            out=big[:, k0:k0 + nk, s * 128:(s + 1) * 128],
                            in_=pt[:, 0:nk * 128].rearrange("p (k c) -> p k c", c=128)),
                            reads=[pk], writes=[("big", k) for k in range(k0, k0 + nk)])
                S.dma("sp", big[:, 12:20, :], ssm_scr[:, :, i * 512:(i + 1) * 512].rearrange("j p t -> p j t"),
                      reads=["ssm_scr"], writes=[("big", k) for k in range(12, 20)])
                att_keys = [("big", k) for k in range(12)]
                ssm_keys = [("big", k) for k in range(12, 20)]
                for m in range(16):
                    wga, k1 = load_fm("gate", m)
                    wao, k2 = load_fm("atto", m)
                    pg, pgk = nf()
                    S.mm(pg[:, :], [(wga[:, k, :], xT[:, k, :]) for k in range(16)], reads=[k1] + kTall(), writes=[pgk])
                    pa_, pak = nf()
                    S.mm(pa_[:, :], [(wao[:, k, :], big[:, k, :]) for k in range(12)], reads=[k2] + att_keys, writes=[pak])
                    s1, s1k = nsg()
                    S.op("act", lambda e, s1=s1, pg=pg: e.activation(out=s1[:], in_=pg[:, :], func=AF.Sigmoid), reads=[pgk], writes=[s1k])
                    S.op("dve", lambda e, s1=s1, pa_=pa_: e.tensor_tensor(out=t1[:], in0=s1[:], in1=pa_[:, :], op=ALU.mult),
                         reads=[s1k, pak], writes=["t1"])
                    wgs, k3 = load_fm("gate", 16 + m)
                    wso, k4 = load_fm("ssmo", m)
                    pg2, pg2k = nf()
                    S.mm(pg2[:, :], [(wgs[:, k, :], xT[:, k, :]) for k in range(16)], reads=[k3] + kTall(), writes=[pg2k])
                    pso, psok = nf()
                    S.mm(pso[:, :], [(wso[:, k, :], big[:, 12 + k, :]) for k in range(8)], reads=[k4] + ssm_keys, writes=[psok])
                    s2, s2k = nsg()
                    S.op("act", lambda e, s2=s2, pg2=pg2: e.activation(out=s2[:], in_=pg2[:, :], func=AF.Sigmoid), reads=[pg2k], writes=[s2k])
                    S.op("dve", lambda e, s2=s2, pso=pso: e.tensor_tensor(out=s2[:], in0=s2[:], in1=pso[:, :], op=ALU.mult),
                         reads=[s2k, psok], writes=[s2k])
                    S.op("dve", lambda e, m=m, s2=s2: e.tensor_tensor(out=big[:, 20 + m, :], in0=t1[:], in1=s2[:], op=ALU.add),
                         reads=["t1", s2k], writes=[("big", 20 + m)])
                mg_keys = [("big", 20 + k) for k in range(16)]
                for c in range(4):
                    w, wk = load_tm("out", c)
                    for s in range(4):
                        ps, pk = nf()
                        S.mm(ps[:, :], [(big[:, 20 + k, s * 128:(s + 1) * 128], w[:, k, :]) for k in range(16)],
                             reads=[wk] + mg_keys, writes=[pk])
                        S.op("dve", lambda e, ps=ps, s=s, c=c: e.scalar_tensor_tensor(
                            out=xbuf[:, s, c * 512:(c + 1) * 512], in0=xbuf[:, s, c * 512:(c + 1) * 512], scalar=ALPHA,
                            in1=ps[:, :], op0=ALU.mult, op1=ALU.add), reads=[pk, kx(s)], writes=[kx(s)])
                layer_norm_tile(1)
                for m in range(64):
                    w, wk = load_fm("up", m)
                    ps, pk = nf()
                    S.mm(ps[:, :], [(w[:, k, :], xT[:, k, :]) for k in range(16)], reads=[wk] + kTall(), writes=[pk])
                    s1, s1k = nsg()
                    S.op("act", lambda e, s1=s1, ps=ps: e.activation(out=s1[:], in_=ps[:, :], func=AF.Relu), reads=[pk], writes=[s1k])
                    S.op("dve", lambda e, s1=s1, m=m: e.tensor_tensor(out=big[:, m, :], in0=s1[:], in1=s1[:], op=ALU.mult),
                         reads=[s1k], writes=[("big", m)])
                h_keys = [("big", k) for k in range(64)]
                for s in range(4):
                    S.dma("sp", pst[:], p_in[i * 512 + s * 128: i * 512 + (s + 1) * 128, :], writes=["pst"])
                    S.op("act", lambda e, s=s: e.copy(out=p16[:], in_=pst[:]), reads=["pst"], writes=["p16"])
                    pt, pk = nb()
                    S.tr([(pt[:, j * 128:(j + 1) * 128], p16[:, j * 128:(j + 1) * 128]) for j in range(2)],
                         identb[:], reads=["p16", "identb"], writes=[pk])
                    S.op("dve", lambda e, pt=pt, s=s: e.tensor_copy(
                        out=pT[:, :, s * 128:(s + 1) * 128], in_=pt[:, 0:256].rearrange("p (k c) -> p k c", c=128)),
                        reads=[pk], writes=[("pT", s)])
                for c in range(4):
                    banks = [nf() for _ in range(4)]
                    for kg in range(4):
                        w, wk = load_tm("down", c * 4 + kg)
                        for s in range(4):
                            S.mm(banks[s][0][:, :], [(big[:, kg * 16 + k, s * 128:(s + 1) * 128], w[:, k, :]) for k in range(16)],
                                 reads=[wk] + h_keys[kg * 16:(kg + 1) * 16], writes=[banks[s][1]], start=(kg == 0), stop=(kg == 3))
                    for s in range(4):
                        S.op("dve", lambda e, s=s, c=c, ps=banks[s][0]: e.scalar_tensor_tensor(
                            out=xbuf[:, s, c * 512:(c + 1) * 512], in0=xbuf[:, s, c * 512:(c + 1) * 512], scalar=ALPHA,
                            in1=ps[:, :], op0=ALU.mult, op1=ALU.add), reads=[banks[s][1], kx(s)], writes=[kx(s)])
                    wpg, k1 = load_tm("pg", c)
                    wpp, k2 = load_tm("pp", c)
                    for s in range(4):
                        pg, pgk = nf()
                        S.mm(pg[:, :], [(xT[:, k, s * 128:(s + 1) * 128], wpg[:, k, :]) for k in range(16)],
                             reads=[k1, kxT(s, 0), kxT(s, 1)], writes=[pgk])
                        pp_, ppk = nf()
                        S.mm(pp_[:, :], [(pT[:, k, s * 128:(s + 1) * 128], wpp[:, k, :]) for k in range(2)],
                             reads=[k2, ("pT", s)], writes=[ppk])
                        s1, s1k = nsg()
                        S.op("act", lambda e, s1=s1, pg=pg: e.activation(out=s1[:], in_=pg[:, :], func=AF.Sigmoid), reads=[pgk], writes=[s1k])
                        S.op("dve", lambda e, s1=s1, pp_=pp_: e.tensor_tensor(out=t1[:], in0=s1[:], in1=pp_[:, :], op=ALU.mult),
                             reads=[s1k, ppk], writes=["t1"])
                        S.op("dve", lambda e, s=s, c=c: e.tensor_tensor(
                            out=xbuf[:, s, c * 512:(c + 1) * 512], in0=xbuf[:, s, c * 512:(c + 1) * 512], in1=t1[:], op=ALU.add),
                            reads=["t1", kx(s)], writes=[kx(s)])
                bigf = big[:].rearrange("p a b -> p (a b)").bitcast(F32)[:, 0:4 * D].rearrange("p (s d) -> p s d", d=D)
                layer_norm_tile(2, tr_dst=False, out_tile=bigf,
                                out_writes=[[("big", 8 * s_ + q_) for q_ in range(8)] for s_ in range(4)])
                for s in range(4):
                    S.dma("act", y_out[i * 512 + s * 128: i * 512 + (s + 1) * 128, :], bigf[:, s, :],
                          reads=[("big", 8 * s + q_) for q_ in range(8)], writes=[("y", s)])
        S.finish()
    return nc


def _shared_maps(NT, ln_emb_g, ln_emb_b, w_in, ssm_lam_re, ssm_lam_im, ssm_log_dt, ssm_b_re, ssm_b_im,
                 ssm_c_re, ssm_c_im, ssm_d, w_glu, b_glu, w_att_o, w_ssm_o, w_out, ln1_g, ln1_b,
                 w_up, w_down, w_ple_gate, w_ple_proj, ln2_g, ln2_b):
    f = np.float32
    m = dict(_consts(NT))
    m["wcat"] = _host_weights(np.asarray(w_in[0], f), np.asarray(w_att_o[0], f), np.asarray(w_ssm_o[0], f),
                              np.asarray(w_out[0], f), np.asarray(w_up[0], f), np.asarray(w_down[0], f),
                              np.asarray(w_ple_gate[0], f), np.asarray(w_ple_proj[0], f), np.asarray(w_glu[0], f))
    g = np.stack([np.asarray(ln_emb_g, f), np.asarray(ln1_g[0], f), np.asarray(ln2_g[0], f)])
    b = np.stack([np.asarray(ln_emb_b, f), np.asarray(ln1_b[0], f), np.asarray(ln2_b[0], f)])
    m["lng"] = np.ascontiguousarray(np.broadcast_to(g[:, None, :], (3, 128, D)))
    m["lnb"] = np.ascontiguousarray(np.broadcast_to(b[:, None, :], (3, 128, D)))

    def dup_p(a):
        t = np.asarray(a, f).transpose(2, 0, 1).reshape(64, 128)
        return np.ascontiguousarray(np.concatenate([t, t], 0))
    m["lamre"] = dup_p(ssm_lam_re[0])
    m["lamim"] = dup_p(ssm_lam_im[0])
    m["logdt"] = np.ascontiguousarray(np.broadcast_to(np.asarray(ssm_log_dt[0], f).reshape(1, 128), (128, 128)))

    def dup_b(a):
        t = np.asarray(a, f).transpose(2, 0, 1, 3)
        return np.ascontiguousarray(np.concatenate([t, t], 0))

    def dup_c(a):
        t = np.asarray(a, f).transpose(3, 0, 1, 2)
        return np.ascontiguousarray(np.concatenate([t, t], 0))
    m["bre"] = dup_b(ssm_b_re[0])
    m["bim"] = dup_b(ssm_b_im[0])
    m["cre"] = dup_c(ssm_c_re[0])
    m["cim"] = dup_c(ssm_c_im[0])
    dsk = np.asarray(ssm_d[0], f).reshape(64, 16)
    m["dsk"] = np.ascontiguousarray(np.broadcast_to(dsk.T[None, :, :], (8, 16, 64)).reshape(128, 64))
    m["bglu"] = np.ascontiguousarray(np.asarray(b_glu[0], f).reshape(8, 128).T)
    return m


def _core_map(shared, NT, x, p, pos, link):
    m = dict(shared)
    NBLK = NT // 512
    m["x"] = np.ascontiguousarray(x, np.float32)
    m["pp_in"] = np.ascontiguousarray(p, np.float32)
    c, s = _rope(pos.astype(np.float32))
    m["ropec"] = np.ascontiguousarray(c)
    m["ropes"] = np.ascontiguousarray(s)
    lk = np.ones((128, NBLK + 1), np.float32)
    lk[:, NBLK // 2] = link
    m["lk"] = lk
    am = shared["amask"].astype(np.float32)
    m["amaskl"] = (am if link else np.zeros_like(am)).astype(ml_dtypes.bfloat16)
    return m


_NC_CACHE = {}


def kernel(x_prompt, x_sample, p_prompt, p_sample, ln_emb_g, ln_emb_b, w_in, ssm_lam_re, ssm_lam_im,
           ssm_log_dt, ssm_b_re, ssm_b_im, ssm_c_re, ssm_c_im, ssm_d, w_glu, b_glu, w_att_o, w_ssm_o,
           w_out, ln1_g, ln1_b, w_up, w_down, w_ple_gate, w_ple_proj, ln2_g, ln2_b):
    NT = 8192
    x_prompt = np.asarray(x_prompt, np.float32)
    x_sample = np.asarray(x_sample, np.float32)
    p_prompt = np.asarray(p_prompt, np.float32)
    p_sample = np.asarray(p_sample, np.float32)
    shared = _shared_maps(NT, ln_emb_g, ln_emb_b, w_in, ssm_lam_re, ssm_lam_im, ssm_log_dt, ssm_b_re, ssm_b_im,
                          ssm_c_re, ssm_c_im, ssm_d, w_glu, b_glu, w_att_o, w_ssm_o, w_out, ln1_g, ln1_b,
                          w_up, w_down, w_ple_gate, w_ple_proj, ln2_g, ln2_b)
    pos2 = np.concatenate([np.arange(4096), np.arange(4096)])
    pos1 = np.arange(8192)
    maps = []
    for c in range(4):
        maps.append(_core_map(shared, NT, x_prompt[2 * c:2 * c + 2].reshape(NT, D),
                              p_prompt[0, 2 * c:2 * c + 2].reshape(NT, 256), pos2, 0.0))
    for c in range(4):
        maps.append(_core_map(shared, NT, x_sample[c], p_sample[0, c], pos1, 1.0))
    if "nc" not in _NC_CACHE:
        _NC_CACHE["nc"] = build(NT)
    res = run_bass_kernel_spmd(_NC_CACHE["nc"], maps, core_ids=list(range(8)))
    ys = [np.asarray(r["y"], np.float32) for r in res.results]
    y_prompt = np.stack([ys[c].reshape(2, 4096, D) for c in range(4)]).reshape(8, 4096, D)
    y_sample = np.stack(ys[4:8]).reshape(4, 8192, D)
    return (y_prompt, y_sample)
```
